# Optimizing a Trainium2 kernel written in Bass

```python
import math
import jax
import jax.numpy as jnp
from jax import lax
import numpy as np

D_MODEL = 1024
BATCH = 4
SEQ = 4096
DEPTH = 1
DEC_BATCH = 2
DEC_SEQ = 16384
PAST_LEN = 128

HEAD_DIM = 64
A_HEADS = D_MODEL // 128
A_KV_HEADS = 2
A_GROUP = A_HEADS // A_KV_HEADS
A_WIDTH = A_HEADS * HEAD_DIM
A_KV_WIDTH = A_KV_HEADS * HEAD_DIM
WINDOW = 128
BLOCK = 128
B_HEADS = D_MODEL // 128
Q_LORA = 3 * D_MODEL // 8
KV_LORA = D_MODEL // 4
NOPE_DIM = 64
ROPE_DIM = 32
V_DIM = 64
B_WIDTH = B_HEADS * V_DIM
ROPE_THETA = 10000.0
EPS = 1e-6

IN_WIDTHS = (A_WIDTH, A_KV_WIDTH, A_KV_WIDTH, A_WIDTH, Q_LORA, KV_LORA, ROPE_DIM, B_WIDTH, 2 * D_MODEL)
IN_SPLITS = tuple(sum(IN_WIDTHS[: i + 1]) for i in range(len(IN_WIDTHS) - 1))
D_IN = sum(IN_WIDTHS)

kernel_name = "hybrid_swa_mla_gated_encoder"


def rms_norm(x, gain):
    x32 = x.astype(jnp.float32)
    y = x32 * lax.rsqrt(jnp.mean(x32 * x32, axis=-1, keepdims=True) + EPS)
    return (y * gain.astype(jnp.float32)).astype(x.dtype)


def alibi_slopes(n):
    start = 2.0 ** (-8.0 / n)
    return start ** jnp.arange(1, n + 1, dtype=jnp.float32)


def rope_tables(S, dtype):
    inv = 1.0 / (ROPE_THETA ** (jnp.arange(0, ROPE_DIM, 2, dtype=jnp.float32) / ROPE_DIM))
    ang = jnp.arange(S, dtype=jnp.float32)[:, None] * inv[None, :]
    return jnp.cos(ang).astype(dtype), jnp.sin(ang).astype(dtype)


def apply_rope(x, cos, sin):
    x1, x2 = jnp.split(x, 2, axis=-1)
    return jnp.concatenate([x1 * cos - x2 * sin, x1 * sin + x2 * cos], axis=-1)


def windowed_gqa(q, k, v, sink):
    B, S, _ = q.shape
    nb = S // BLOCK
    q = q.reshape(B, nb, BLOCK, A_KV_HEADS, A_GROUP, HEAD_DIM) * (HEAD_DIM ** -0.5)
    pad = ((0, 0), (BLOCK, BLOCK), (0, 0), (0, 0))
    kp = jnp.pad(k.reshape(B, S, A_KV_HEADS, HEAD_DIM), pad).reshape(B, nb + 2, BLOCK, A_KV_HEADS, HEAD_DIM)
    vp = jnp.pad(v.reshape(B, S, A_KV_HEADS, HEAD_DIM), pad).reshape(B, nb + 2, BLOCK, A_KV_HEADS, HEAD_DIM)
    kb = jnp.concatenate([kp[:, :-2], kp[:, 1:-1], kp[:, 2:]], axis=2)
    vb = jnp.concatenate([vp[:, :-2], vp[:, 1:-1], vp[:, 2:]], axis=2)
    s = jnp.einsum("bnqhgd,bnkhd->bnhgqk", q, kb).astype(jnp.float32)
    qi = jnp.arange(BLOCK)
    kj = jnp.arange(3 * BLOCK)
    rel = kj[None, :] - BLOCK - qi[:, None]
    kpos = jnp.arange(nb)[:, None] * BLOCK - BLOCK + kj[None, :]
    valid = (jnp.abs(rel) <= WINDOW)[None] & ((kpos >= 0) & (kpos < S))[:, None, :]
    dist = jnp.abs(rel).astype(jnp.float32)
    bias = (-alibi_slopes(A_HEADS)[:, None, None] * dist[None]).reshape(A_KV_HEADS, A_GROUP, BLOCK, 3 * BLOCK)
    s = jnp.where(valid[None, :, None, None], s + bias[None, None], -1e30)
    sink_l = sink.astype(jnp.float32).reshape(A_KV_HEADS, A_GROUP)[None, None, :, :, None, None]
    m = jnp.maximum(jnp.max(s, axis=-1, keepdims=True), sink_l)
    p = jnp.exp(s - m)
    denom = jnp.sum(p, axis=-1, keepdims=True) + jnp.exp(sink_l - m)
    p = (p / denom).astype(vb.dtype)
    o = jnp.einsum("bnhgqk,bnkhd->bnqhgd", p, vb)
    return o.reshape(B, S, A_WIDTH)


def mla(cq, ckv, kr, g_q, w_uq, g_kv, w_ukv):
    B, S, _ = cq.shape
    nb = S // BLOCK
    q = (rms_norm(cq, g_q) @ w_uq).reshape(B, S, B_HEADS, NOPE_DIM + ROPE_DIM)
    q_nope, q_rope = q[..., :NOPE_DIM], q[..., NOPE_DIM:]
    kv = (rms_norm(ckv, g_kv) @ w_ukv).reshape(B, S, B_HEADS, NOPE_DIM + V_DIM)
    k_nope, v = kv[..., :NOPE_DIM], kv[..., NOPE_DIM:]
    cos, sin = rope_tables(S, cq.dtype)
    q_rope = apply_rope(q_rope, cos[:, None, :], sin[:, None, :])
    k_rope = apply_rope(kr, cos, sin)
    scale = (NOPE_DIM + ROPE_DIM) ** -0.5
    q = jnp.concatenate([q_nope, q_rope], axis=-1) * scale
    k = jnp.concatenate([k_nope, jnp.broadcast_to(k_rope[:, :, None, :], (B, S, B_HEADS, ROPE_DIM))], axis=-1)
    qb = q.reshape(B, nb, BLOCK, B_HEADS, NOPE_DIM + ROPE_DIM).transpose(1, 0, 2, 3, 4)

    def attend(q_blk):
        s = jnp.einsum("bqhd,bkhd->bhqk", q_blk, k).astype(jnp.float32)
        p = jax.nn.softmax(s, axis=-1).astype(v.dtype)
        return jnp.einsum("bhqk,bkhd->bqhd", p, v)

    o = lax.map(attend, qb)
    return o.transpose(1, 0, 2, 3, 4).reshape(B, S, B_WIDTH)


def encoder_layer(x, c, w_ada, b_ada, g_norm, w_in, g_q, w_uq, g_kv, w_ukv, sink, w_oa, w_ob, w_out):
    mod = jax.nn.silu(c) @ w_ada + b_ada
    shift, scale, gate_res = jnp.split(mod, 3, axis=-1)
    h = rms_norm(x, g_norm) * (1.0 + scale[:, None, :]) + shift[:, None, :]
    proj = h @ w_in
    qa, ka, va, za, cq, ckv, kr, zb, gm = jnp.split(proj, IN_SPLITS, axis=-1)
    ya = windowed_gqa(qa, ka, va, sink) * jax.nn.silu(za)
    yb = mla(cq, ckv, kr, g_q, w_uq, g_kv, w_ukv) * jax.nn.silu(zb)
    ga, gb = jnp.split(jax.nn.sigmoid(gm), 2, axis=-1)
    merged = ga * (ya @ w_oa) + gb * (yb @ w_ob)
    return x + gate_res[:, None, :] * (merged @ w_out)


def setup_inputs(seed: int = 0) -> dict:
    key = jax.random.key(seed)
    ks = jax.random.split(key, 20)
    f32 = jnp.float32
    nrm = lambda k, shape, s: jax.random.normal(k, shape, f32) * s
    return {
        "x_prompt": nrm(ks[0], (BATCH, SEQ, D_MODEL), 1.0),
        "x_sample": nrm(ks[1], (DEC_BATCH, DEC_SEQ, D_MODEL), 1.0),
        "c_prompt": nrm(ks[2], (BATCH, D_MODEL), 1.0),
        "c_sample": nrm(ks[3], (DEC_BATCH, D_MODEL), 1.0),
        "w_ada": nrm(ks[4], (DEPTH, D_MODEL, 3 * D_MODEL), 0.5 * D_MODEL ** -0.5),
        "b_ada": nrm(ks[5], (DEPTH, 3 * D_MODEL), 0.01),
        "g_norm": 1.0 + nrm(ks[6], (DEPTH, D_MODEL), 0.01),
        "w_in": nrm(ks[7], (DEPTH, D_MODEL, D_IN), D_MODEL ** -0.5),
        "g_q": 1.0 + nrm(ks[8], (DEPTH, Q_LORA), 0.01),
        "w_uq": nrm(ks[9], (DEPTH, Q_LORA, B_HEADS * (NOPE_DIM + ROPE_DIM)), Q_LORA ** -0.5),
        "g_kv": 1.0 + nrm(ks[10], (DEPTH, KV_LORA), 0.01),
        "w_ukv": nrm(ks[11], (DEPTH, KV_LORA, B_HEADS * (NOPE_DIM + V_DIM)), KV_LORA ** -0.5),
        "sink": nrm(ks[12], (DEPTH, A_HEADS), 0.5),
        "w_oa": nrm(ks[13], (DEPTH, A_WIDTH, D_MODEL), A_WIDTH ** -0.5),
        "w_ob": nrm(ks[14], (DEPTH, B_WIDTH, D_MODEL), B_WIDTH ** -0.5),
        "w_out": nrm(ks[15], (DEPTH, D_MODEL, D_MODEL), D_MODEL ** -0.5),
        "g_final": 1.0 + nrm(ks[16], (D_MODEL,), 0.01),
    }


def reference(x_prompt, x_sample, c_prompt, c_sample, w_ada, b_ada, g_norm, w_in, g_q, w_uq, g_kv, w_ukv,
              sink, w_oa, w_ob, w_out, g_final):
    def trunk(x, c):
        for l in range(DEPTH):
            x = encoder_layer(x, c, w_ada[l], b_ada[l], g_norm[l], w_in[l], g_q[l], w_uq[l], g_kv[l], w_ukv[l],
                              sink[l], w_oa[l], w_ob[l], w_out[l])
        return rms_norm(x, g_final)

    y_prompt = trunk(x_prompt, c_prompt)
    y_sample = trunk(x_sample, c_sample)
    return (y_prompt, y_sample)
```

```python
import numpy as np
from contextlib import ExitStack
import concourse.bass as bass
import concourse.mybir as mybir
from concourse.bass_utils import run_bass_kernel_spmd

F32 = mybir.dt.float32
BF = mybir.dt.bfloat16
AF = mybir.ActivationFunctionType
ALU = mybir.AluOpType
EPS = 1e-6
NCORES = 8
import os as _os
EV_SPLIT = int(_os.environ.get('K_EVSPLIT', 8))
ENGS = ("pe", "act", "dve", "pool", "sp")


class _Op:
    __slots__ = ("eng", "fn", "deps", "dma", "sem", "val", "signal", "phase")


class Sched:
    def __init__(self, nc, esems, dsems):
        self.nc = nc
        self.esem = esems
        self.dsems = dsems
        self.ndma = {q: 0 for q in dsems}
        self.dma_last = {q: [None] * len(dsems[q]) for q in dsems}
        self.state = {}
        self.ops = []
        self.cnt = {e: 0 for e in ENGS}
        self.known = {e: {} for e in ENGS}
        self.phase = 0

    maxops = None
    nops = 0

    def add(self, eng, fn, r=(), w=(), dma=False):
        Sched.nops += 1
        if Sched.maxops is not None and Sched.nops > Sched.maxops:
            return None
        op = _Op()
        op.eng, op.fn, op.dma, op.signal, op.phase = eng, fn, dma, False, self.phase
        op.sem = None
        op.val = 0
        deps = {}
        for x in r:
            st = self.state.get(x)
            if st is not None and st[0] is not None:
                deps[id(st[0])] = st[0]
        for x in w:
            st = self.state.get(x)
            if st is not None:
                if st[0] is not None:
                    deps[id(st[0])] = st[0]
                for o in st[1].values():
                    deps[id(o)] = o
                for o in st[2]:
                    deps[id(o)] = o
        if dma:
            pool_ = self.dsems[eng]
            slot = self.ndma[eng] % len(pool_)
            prev = self.dma_last[eng][slot]
            if prev is not None:
                deps[id(prev)] = prev
            op.sem = pool_[slot]
            op.val = (self.ndma[eng] // len(pool_) + 1) * 16
            self.dma_last[eng][slot] = op
            self.ndma[eng] += 1
        op.deps = [d for d in deps.values()
                   if d.phase == self.phase and not (not d.dma and not dma and d.eng == eng == "pe")]
        for d in op.deps:
            d.signal = True
        for x in r:
            st = self.state.setdefault(x, [None, {}, []])
            if dma:
                st[2].append(op)
            else:
                st[1][eng] = op
        for x in w:
            self.state[x] = [op, {}, []]
        self.ops.append(op)
        return op

    def emit(self):
        ops = self.ops
        for op in ops:
            if not op.dma and op.signal:
                self.cnt[op.eng] += 1
                op.sem = self.esem[op.eng]
                op.val = self.cnt[op.eng]
        per = {e: [] for e in ENGS}
        for op in ops:
            per[op.eng].append(op)
        phase = self.phase

        def run(eng, e):
            known = self.known[eng]
            for op in per[eng]:
                for d in op.deps:
                    if known.get(d.sem, 0) < d.val:
                        e.wait_ge(d.sem, d.val)
                        known[d.sem] = d.val
                ins = op.fn(e)
                if op.dma:
                    ins.then_inc(op.sem, 16)
                elif op.signal:
                    ins.then_inc(op.sem, 1)
            if eng == "sp":
                for lst in self.dma_last.values():
                    for last in lst:
                        if last is not None and last.phase == phase and known.get(last.sem, 0) < last.val:
                            e.wait_ge(last.sem, last.val)
                            known[last.sem] = last.val

        with self.nc.Block() as block:
            @block.tensor
            def _(e):
                run("pe", e)

            @block.scalar
            def _(e):
                run("act", e)

            @block.vector
            def _(e):
                run("dve", e)

            @block.gpsimd
            def _(e):
                run("pool", e)

            @block.sync
            def _(e):
                run("sp", e)
        self.phase += 1
        self.ops = []


def build(S_s, S_p, stop=None):
    nc = bass.Bass("TRN2", target_bir_lowering=False)
    seqs = [dict(n=0, nm="s", S=S_s, Nq=S_s // 4), dict(n=1, nm="p", S=S_p, Nq=S_p // 2)]
    Smax = max(S_s, S_p)
    Nqmax = max(q["Nq"] for q in seqs)

    def din(name, shape, dt=F32):
        return nc.dram_tensor(name, list(shape), dt, kind="ExternalInput").ap()

    def dscr(name, shape, dt=BF):
        if _os.environ.get("K_DBG") and name[:2] in ("Ks", "Kr", "Vs"):
            return nc.dram_tensor(name, list(shape), dt, kind="ExternalOutput").ap()
        return nc.dram_tensor(name, list(shape), dt).ap()

    for q in seqs:
        nm, S, Nq = q["nm"], q["S"], q["Nq"]
        NT = Nq // 128 + 2
        q["NT"] = NT
        q["xf"] = din("xf_" + nm, [S, 1024])
        q["xo"] = din("xo_" + nm, [Nq + 256, 1024])
        q["ropeK"] = din("ropeK_" + nm, [S, 64])
        q["ropeQ"] = din("ropeQ_" + nm, [128, 2, Nq])
        q["wmask"] = din("wmask_" + nm, [128, 2])
        q["y"] = nc.dram_tensor("y_" + nm, [Nq, 1024], F32, kind="ExternalOutput").ap()
        q["Ks"] = dscr("Ks_" + nm, [8, 64, S])
        q["Kr"] = dscr("Kr_" + nm, [32, S])
        q["Vs"] = dscr("Vs_" + nm, [8, 128, S // 128, 128])
        q["Qs"] = dscr("Qs_" + nm, [8, 96, Nq])
        q["Kw"] = dscr("Kw_" + nm, [2, 128, NT * 128])
        q["Vw"] = dscr("Vw_" + nm, [128, NT, 256])
        q["Ybs"] = dscr("Ybs_" + nm, [4, 128, Nq])
        q["Hs"] = dscr("Hs_" + nm, [8, 128, Nq])
    cT_d = din("cT", [128, 8, 2])
    w_ada_d = din("w_ada", [1024, 3072])
    b_adaT_d = din("b_adaT", [128, 24])
    b_gate_d = din("b_gate", [1, 1024])
    b_scale_d = din("b_scale", [1, 1024])
    gn_bc_d = din("gn_bc", [128, 1024])
    g_normT_d = din("g_normT", [128, 8])
    w_inA_d = din("w_inA", [1024, 320])
    w_inB_d = din("w_inB", [1024, 768])
    w_inD_d = din("w_inD", [1024, 3584])
    gkv_d = din("gkv_bc", [128, 256])
    gq_d = din("gq_bc", [128, 384])
    gf_d = din("gf_bc", [128, 1024])
    w_uk_d = din("w_uk", [256, 512])
    w_uv_d = din("w_uv", [256, 512])
    w_uqn_d = din("w_uqn", [384, 512])
    w_uqr_d = din("w_uqr", [384, 256])
    w_uqrs_d = din("w_uqrs", [384, 256])
    sink_d = din("sink_bc", [128, 8])
    w_oa_d = din("w_oa", [512, 1024])
    w_ob_d = din("w_ob", [512, 1024])
    w_out_d = din("w_out", [1024, 1024])
    wbias_d = din("wbias", [128, 3072])
    ident_d = din("ident", [128, 128])

    with ExitStack() as top:
        uid = [0]

        def sb(st, name, shape, dt):
            uid[0] += 1
            return st.enter_context(nc.sbuf_tensor("%s_%d" % (name, uid[0]), list(shape), dt))

        def psb(st, name):
            uid[0] += 1
            return st.enter_context(nc.psum_tensor("%s_%d" % (name, uid[0]), [128, 512], F32))

        esems = {e: top.enter_context(nc.semaphore("sem_" + e)) for e in ENGS}
        dsems = {"sp": [top.enter_context(nc.semaphore("dsem%d" % i)) for i in range(int(_os.environ.get("K_NSEM", 12)))],
                 "pool": [top.enter_context(nc.semaphore("psem%d" % i)) for i in range(4)]}
        sch = Sched(nc, esems, dsems)
        A = sch.add

        def mm(out, lhsT, rhs, start, stop, r, w):
            return A("pe", lambda e: e.matmul(out, lhsT, rhs, start=start, stop=stop), r, w)

        def tp(out, in_, r, w):
            return A("pe", lambda e: e.transpose(out, in_, ident_bf[:, :]), r, w)

        def act(out, in_, func, r, w, **kw):
            return A("act", lambda e: e.activation(out, in_, func, **kw), r, w)

        def ts(eng, out, in0, s1, s2, op0, op1, r, w):
            if op1 is None:
                return A(eng, lambda e: e.tensor_scalar(out=out, in0=in0, scalar1=s1, scalar2=None, op0=op0), r, w)
            return A(eng, lambda e: e.tensor_scalar(out=out, in0=in0, scalar1=s1, scalar2=s2, op0=op0, op1=op1), r, w)

        def tt(eng, out, in0, in1, op, r, w):
            return A(eng, lambda e: e.tensor_tensor(out=out, in0=in0, in1=in1, op=op), r, w)

        def stt(out, in0, scalar, in1, op0, op1, r, w):
            return A("dve", lambda e: e.scalar_tensor_tensor(out=out, in0=in0, scalar=scalar, in1=in1, op0=op0, op1=op1), r, w)

        def cp(eng, out, in_, r, w):
            if eng == "act":
                return A("act", lambda e: e.copy(out, in_), r, w)
            return A(eng, lambda e: e.tensor_copy(out=out, in_=in_), r, w)

        def recip(out, in_, r, w):
            return A("dve", lambda e: e.reciprocal(out=out, in_=in_), r, w)

        def mset(eng, ap, val, w):
            return A(eng, lambda e: e.memset(ap, val), (), w)

        def dma(out, in_, r, w, q="sp"):
            return A(q, lambda e: e.dma_start(out=out, in_=in_), r, w, dma=True)

        def rstd_ops(stat, s, inv_n, key):
            ts("pool", stat[:, s, 1:2], stat[:, s, 0:1], inv_n, EPS, ALU.mult, ALU.add, [(key, s, 0)], [(key, s, 1)])
            tt("pool", stat[:, s, 2:3], stat[:, s, 1:2], mhalf[:, 0:1], ALU.pow, [(key, s, 1)], [(key, s, 2)])

        ident_bf = sb(top, "ident_bf", [128, 128], BF)
        a_all = sb(top, "a_all", [128, 8, 2], F32)
        b_all = sb(top, "b_all", [128, 8, 2], F32)
        gate_bc = sb(top, "gate_bc", [128, 2, 1024], F32)
        gf = sb(top, "gf", [128, 1024], F32)
        wb = sb(top, "wb", [128, 3072], BF)
        es = sb(top, "es", [128, 8], F32)
        mhalf = sb(top, "mhalf", [128, 8], F32)
        gkv = sb(top, "gkv", [128, 256], F32)
        gq = sb(top, "gq", [128, 384], F32)
        b_bf = sb(top, "b_bf", [128, 8, 2], BF)
        ones_row = sb(top, "ones_row", [1, 512], BF)
        stAB = top.enter_context(ExitStack())
        a_bc = sb(stAB, "a_bc", [128, 2, 1024], F32)

        with ExitStack() as st:
            wa = [sb(st, "wa%d" % i, [128, 8, 1024], F32) for i in range(2)]
            cT = sb(st, "cTt", [128, 16], F32)
            ch = sb(st, "ch", [128, 16], F32)
            th = sb(st, "th", [128, 16], F32)
            sc = sb(st, "sc", [128, 16], F32)
            sc_rep = sb(st, "sc_rep", [128, 16, 128], F32)
            badaT = sb(st, "badaT", [128, 24], F32)
            bgate = sb(st, "bgate", [1, 1024], F32)
            gnT = sb(st, "gnT", [128, 8], F32)
            sink_t = sb(st, "sink_t", [128, 8], F32)
            ones_f = sb(st, "ones_f", [1, 128], F32)
            bscale = sb(st, "bscale", [1, 1024], F32)
            gn_bc = sb(st, "gn_bc", [128, 1024], F32)
            mod_sb = sb(st, "mod_sb", [128, 16, 2], F32)
            ps_mod = psb(st, "ps_mod")
            ps_g = [[psb(st, "ps_g%d%d" % (n, h)) for h in range(2)] for n in range(2)]

            dma(ident_bf[:, :], ident_d, [], ["ident"], q="pool")
            dma(wb[:, :], wbias_d, [], ["wb"], q="pool")
            dma(cT[:, :], cT_d.rearrange("p j n -> p (j n)"), [], ["cT"])
            dma(badaT[:, :], b_adaT_d, [], ["badaT"])
            dma(bgate[:, :], b_gate_d, [], ["bgate"])
            dma(gnT[:, :], g_normT_d, [], ["gnT"])
            dma(bscale[:, :], b_scale_d, [], ["bscale"])
            dma(gn_bc[:, :], gn_bc_d, [], ["gn_bc"])
            mset("pool", ones_row[:, :], 1.0, ["ones_row"])
            dma(sink_t[:, :], sink_d, [], ["sink"])
            dma(gf[:, :], gf_d, [], ["gf"])
            dma(gkv[:, :], gkv_d, [], ["gkv"])
            dma(gq[:, :], gq_d, [], ["gq"])
            mset("pool", mhalf[:, :], -0.5, ["mhalf"])
            mset("pool", ones_f[:, :], 1.0, ["ones_f"])
            def ldwa(third):
                dma(wa[third % 2][:, :, :],
                    w_ada_d[:, third * 1024:(third + 1) * 1024].rearrange("(k p) n -> p k n", p=128),
                    [], [("wa", third % 2)])
            ldwa(0)
            ldwa(1)
            ts("dve", ch[:, :], cT[:, :], 0.5, None, ALU.mult, None, ["cT"], ["ch"])
            act(th[:, :], ch[:, :], AF.Tanh, ["ch"], ["th"])
            stt(sc[:, :], th[:, :], 1.0, ch[:, :], ALU.add, ALU.mult, ["th", "ch"], ["sc"])
            cp("dve", sc_rep[:, :, :], sc[:, :].unsqueeze(2).broadcast_to([128, 16, 128]), ["sc"], ["sc_rep"])
            act(es[:, :], sink_t[:, :], AF.Exp, ["sink"], ["es"])
            pm = ps_mod[:, 0:32].rearrange("p (a n) -> p a n", n=2)
            for third in range(2):
                for j in range(8):
                    for k in range(8):
                        mm(pm[:, third * 8 + j, :], wa[third][:, k, j * 128:(j + 1) * 128], sc[:, 2 * k:2 * k + 2],
                           k == 0, k == 7, [("wa", third), "sc"], ["ps_mod"])
                if third == 0:
                    ldwa(2)
            for n in range(2):
                for h in range(2):
                    for k in range(8):
                        mm(ps_g[n][h][:, :], sc_rep[:, 2 * k + n, :], wa[0][:, k, h * 512:(h + 1) * 512],
                           k == 0, False, ["sc_rep", ("wa", 0)], [("ps_g", n, h)])
                    mm(ps_g[n][h][:, :], ones_f[0:1, :], bgate[0:1, h * 512:(h + 1) * 512], False, True,
                       ["ones_f", "bgate"], [("ps_g", n, h)])
                    ts("dve", gate_bc[:, n, h * 512:(h + 1) * 512], ps_g[n][h][:, :], 0.25, None, ALU.mult, None,
                       [("ps_g", n, h)], ["gate_bc"])
            for n in range(2):
                for h in range(2):
                    for k in range(8):
                        mm(ps_g[n][h][:, :], sc_rep[:, 2 * k + n, :], wa[1][:, k, h * 512:(h + 1) * 512],
                           k == 0, False, ["sc_rep", ("wa", 1)], [("ps_g", n, h)])
                    mm(ps_g[n][h][:, :], ones_f[0:1, :], bscale[0:1, h * 512:(h + 1) * 512], False, True,
                       ["ones_f", "bscale"], [("ps_g", n, h)])
                    stt(a_bc[:, n, h * 512:(h + 1) * 512], ps_g[n][h][:, :], 1.0, gn_bc[:, h * 512:(h + 1) * 512], ALU.add, ALU.mult,
                        [("ps_g", n, h), "gn_bc"], ["a_bc"])
            tt("dve", mod_sb[:, :, :], pm[:, 0:16, :], badaT[:, 0:16].unsqueeze(2).broadcast_to([128, 16, 2]), ALU.add,
               ["ps_mod", "badaT"], ["mod_sb"])
            cp("dve", b_all[:, :, :], mod_sb[:, 0:8, :], ["mod_sb"], ["b_all"])
            cp("dve", b_bf[:, :, :], mod_sb[:, 0:8, :], ["mod_sb"], ["b_bf"])
            stt(a_all[:, :, :], mod_sb[:, 8:16, :], 1.0, gnT[:, :].unsqueeze(2).broadcast_to([128, 8, 2]), ALU.add, ALU.mult,
                ["mod_sb", "gnT"], ["a_all"])
            sch.emit()
            if stop == 0:
                return nc

        def front(n, X, rx, stat, s, xn, rxn, pT, rpT, junk, dst, rdst):
            act(junk[:, :], X, AF.Square, [rx], [("st", s, 0)], accum_out=stat[:, s, 0:1])
            rstd_ops(stat, s, 1.0 / 1024, "st")
            ts("dve", xn[:, :], X, stat[:, s, 2:3], None, ALU.mult, None, [rx, ("st", s, 2)], [rxn])
            pTv = pT[:, :].bitcast(BF)
            for j in range(8):
                tp(pTv[:, j * 128:(j + 1) * 128], xn[:, j * 128:(j + 1) * 128], [rxn], [rpT])
            for j in range(8):
                if j < EV_SPLIT:
                    act(dst(j), pTv[:, j * 128:(j + 1) * 128], AF.Identity, [rpT, "a_all", "b_all"], [rdst(j)],
                        scale=a_all[:, j, n:n + 1], bias=b_all[:, j, n:n + 1])
                else:
                    ts("dve", dst(j), pTv[:, j * 128:(j + 1) * 128], a_all[:, j, n:n + 1], b_all[:, j, n:n + 1],
                       ALU.mult, ALU.add, [rpT, "a_all", "b_all"], [rdst(j)])

        def swpipe(items, stages):
            n_, m_ = len(items), len(stages)
            if _os.environ.get("K_NOSKEW"):
                for it in items:
                    for f_ in stages:
                        if f_ is not None:
                            f_(it)
                return
            for k in range(n_ + m_ - 1):
                for s_ in range(m_ - 1, -1, -1):
                    i_ = k - s_
                    if 0 <= i_ < n_ and stages[s_] is not None:
                        stages[s_](items[i_])

        with ExitStack() as st:
            wA = sb(st, "wA", [128, 8, 320], BF)
            wuk = sb(st, "wuk", [128, 2, 512], BF)
            wuv = sb(st, "wuv", [128, 2, 512], BF)
            dma(wA[:, :, :], w_inA_d.rearrange("(k p) n -> p k n", p=128), [], ["wA"], q="pool")
            dma(wuk[:, :, :], w_uk_d.rearrange("(k p) n -> p k n", p=128), [], ["wuk"], q="pool")
            dma(wuv[:, :, :], w_uv_d.rearrange("(k p) n -> p k n", p=128), [], ["wuv"], q="pool")
            NX = 6
            xt = [sb(st, "xt%d" % i, [128, 1024], F32) for i in range(NX)]
            junk = sb(st, "junk", [128, 1024], BF)
            stat = sb(st, "stat", [128, 16, 4], F32)
            stat2 = sb(st, "stat2", [128, 16, 4], F32)
            xn = [sb(st, "xn%d" % i, [128, 1024], BF) for i in range(2)]
            hT = [sb(st, "hT%d" % i, [128, 8, 128], BF) for i in range(2)]
            ckvf = [sb(st, "ckvf%d" % i, [128, 256], F32) for i in range(3)]
            ckvn = [sb(st, "ckvn%d" % i, [128, 288], BF) for i in range(2)]
            r1 = [sb(st, "r1_%d" % i, [128, 32], F32) for i in range(4)]
            r2 = [sb(st, "r2_%d" % i, [128, 32], F32) for i in range(4)]
            ckvT = [sb(st, "ckvT%d" % i, [128, 2, 512], BF) for i in range(2)]
            krT = [sb(st, "krT%d" % i, [32, 512], BF) for i in range(2)]
            kst = [sb(st, "kst%d" % i, [128, 4, 512], BF) for i in range(2)]
            vst = [sb(st, "vst%d" % i, [128, 8, 4, 128], BF) for i in range(2)]
            ropeK = [sb(st, "ropeK%d" % i, [128, q["S"] // 128, 64], F32) for i, q in enumerate(seqs)]
            bA = sb(st, "bA", [1, 2, 320], BF)
            pT = [psb(st, "pT%d" % i) for i in range(2)]
            pL = [psb(st, "pL%d" % i) for i in range(2)]
            pT2l = [psb(st, "pT2_%d" % i) for i in range(2)]
            pK = [psb(st, "pK%d" % i) for i in range(1)] * 2
            pV = psb(st, "pV")
            for b in range(2):
                for tl in range(4):
                    mset("pool", vst[b][:, :, tl, 64:128], 1.0, [("vst1", b)])
            for n in range(2):
                for j in range(8):
                    mm(pK[0][0:1, 0:320], b_bf[:, j, n:n + 1], wA[:, j, :], j == 0, j == 7, ["b_bf", "wA"], [("pK", 0)])
                cp("act", bA[0:1, n, :], pK[0][0:1, 0:320], [("pK", 0)], ["bA"])
            for q in seqs:
                dma(ropeK[q["n"]][:, :, :], q["ropeK"].rearrange("(t p) c -> p t c", p=128), [], [("ropeK", q["n"])])
            tiles = []
            for q in seqs:
                for t in range(q["S"] // 128):
                    i = len(tiles)
                    tiles.append(dict(i=i, q=q, n=q["n"], t=t, g=t // 4, tl=t % 4, gb=(i // 4) % 2))
            p2l = [pT2l[0][:, :].bitcast(BF), pT2l[1][:, :].bitcast(BF)]

            def sL(T):
                x = T["i"] % NX
                dma(xt[x][:, :], T["q"]["xf"][T["t"] * 128:(T["t"] + 1) * 128, :], [], [("xt", x)])

            def s1(T):
                x, s_ = T["i"] % NX, T["i"] % 16
                act(junk[:, :], xt[x][:, :], AF.Square, [("xt", x)], [("st", s_, 0)], accum_out=stat[:, s_, 0:1])

            def s2(T):
                rstd_ops(stat, T["i"] % 16, 1.0 / 1024, "st")

            def s3(T):
                i, n = T["i"], T["n"]
                x, s_, b = i % NX, i % 16, i % 2
                stt(xn[b][:, :], xt[x][:, :], stat[:, s_, 2:3], a_bc[:, n, :], ALU.mult, ALU.mult,
                    [("xt", x), ("st", s_, 2), "a_bc"], [("xn", b)])

            def s4(T):
                b = T["i"] % 2
                pTv = pT[b][:, :].bitcast(BF)
                for j in range(8):
                    tp(pTv[:, j * 128:(j + 1) * 128], xn[b][:, j * 128:(j + 1) * 128], [("xn", b)], [("pT", b)])

            def s5(T):
                b = T["i"] % 2
                cp("act", hT[b][:, :, :].rearrange("p j t -> p (j t)"), pT[b][:, :].bitcast(BF), [("pT", b)], [("hT", b)])

            def s6(T):
                b, n, bl = T["i"] % 2, T["n"], T["i"] % 2
                for j in range(8):
                    mm(pL[bl][:, 0:320], hT[b][:, j, :], wA[:, j, :], j == 0, False, [("hT", b), "wA"], [("pL", bl)])
                mm(pL[bl][:, 0:320], ones_row[0:1, 0:128], bA[0:1, n, :], False, True, ["ones_row", "bA"], [("pL", bl)])

            def s7(T):
                i = T["i"]
                b, s_ = i % 2, i % 16
                act(junk[:, 0:256], pL[b][:, 0:256], AF.Square, [("pL", b)], [("st2", s_, 0), ("pLq", b)], accum_out=stat2[:, s_, 0:1])

            def s8(T):
                i, n, t = T["i"], T["n"], T["t"]
                b, c3, c4 = i % 2, i % 3, i % 4
                rstd_ops(stat2, i % 16, 1.0 / 256, "st2")
                rr = [("pL", b), ("pLq", b), ("ropeK", n)]
                tt("dve", r1[c4][:, :], pL[b][:, 256:288], ropeK[n][:, t, 0:32], ALU.mult, rr, [("r1", c4)])
                tt("dve", r2[c4][:, :], pL[b][:, 288:320], ropeK[n][:, t, 32:64], ALU.mult, rr, [("r2", c4)])
                cp("dve", ckvf[c3][:, :], pL[b][:, 0:256], rr, [("ckvf", c3)])

            def s9(T):
                i = T["i"]
                b, s_, c3, c4 = i % 2, i % 16, i % 3, i % 4
                stt(ckvn[b][:, 0:256], ckvf[c3][:, :], stat2[:, s_, 2:3], gkv[:, :], ALU.mult, ALU.mult,
                    [("ckvf", c3), ("st2", s_, 2), "gkv"], [("ckvn", b, 0)])
                tt("pool", ckvn[b][:, 256:288], r1[c4][:, :], r2[c4][:, :], ALU.add, [("r1", c4), ("r2", c4)], [("ckvn", b, 1)])

            def s10(T):
                b = T["i"] % 2
                p2 = p2l[b]
                rck = [("ckvn", b, 0), ("ckvn", b, 1)]
                tp(p2[:, 0:128], ckvn[b][:, 0:128], rck, [("pT2", b)])
                tp(p2[:, 128:256], ckvn[b][:, 128:256], rck, [("pT2", b)])
                tp(p2[0:32, 256:384], ckvn[b][:, 256:288], rck, [("pT2", b)])

            def s11(T):
                b, gb, tl = T["i"] % 2, T["gb"], T["tl"]
                p2 = p2l[b]
                cp("act", ckvT[gb][:, :, tl * 128:(tl + 1) * 128],
                   p2[:, 0:256].rearrange("p (c t) -> p c t", c=2), [("pT2", b)], [("ckvT", gb, tl)])
                cp("act", krT[gb][0:32, tl * 128:(tl + 1) * 128], p2[0:32, 256:384], [("pT2", b)], [("krT", gb, tl)])

            def s12(T):
                gb, tl = T["gb"], T["tl"]
                for c in range(2):
                    mm(pV[:, :], ckvT[gb][:, c, tl * 128:(tl + 1) * 128], wuv[:, c, :], c == 0, c == 1,
                       [("ckvT", gb, tl), "wuv"], ["pV"])

            def s13(T):
                gb, tl = T["gb"], T["tl"]
                cp("dve", vst[gb][:, :, tl, 0:64], pV[:, :].rearrange("p (h c) -> p h c", h=8),
                   ["pV", ("vst1", gb)], [("vst", gb, tl)])

            def s14(T):
                if T["tl"] != 3:
                    return
                gb = T["gb"]
                rall = [("ckvT", gb, i) for i in range(4)]
                for p in range(4):
                    for c in range(2):
                        mm(pK[0][:, :], wuk[:, c, p * 128:(p + 1) * 128], ckvT[gb][:, c, :], c == 0, c == 1,
                           rall + ["wuk"], [("pK", 0)])
                    cp("act" if p % 2 == 0 else "dve", kst[gb][:, p, :], pK[0][:, :], [("pK", 0)], [("kst", gb, p)])

            def s15(T):
                if T["tl"] != 3:
                    return
                q, g, gb = T["q"], T["g"], T["gb"]
                dma(q["Ks"].rearrange("h d t -> (h d) t")[:, g * 512:(g + 1) * 512].rearrange("(p r) t -> r p t", r=128),
                    kst[gb][:, :, :], [("kst", gb, p) for p in range(4)], [])
                dma(q["Kr"][:, g * 512:(g + 1) * 512], krT[gb][0:32, :], [("krT", gb, i) for i in range(4)], [])
                dma(q["Vs"][:, :, 4 * g:4 * g + 4, :].rearrange("h p b c -> p h b c"), vst[gb][:, :, :, :],
                    [("vst", gb, i) for i in range(4)], [])

            swpipe(tiles, [sL, None, s1, s2, s3, s4, s5, s6, s7, s8, s9, s10, s11, s12, s13, s14, s15])
            sch.emit()
            if stop == 1:
                return nc
        stAB.close()

        with ExitStack() as st:
            wB = sb(st, "wB", [128, 8, 768], BF)
            wuqn = sb(st, "wuqn", [128, 3, 512], BF)
            wuqr = sb(st, "wuqr", [128, 3, 256], BF)
            wuqrs = sb(st, "wuqrs", [128, 3, 256], BF)
            dma(wB[:, :, :], w_inB_d.rearrange("(k p) n -> p k n", p=128), [], ["wB"], q="pool")
            dma(wuqn[:, :, :], w_uqn_d.rearrange("(k p) n -> p k n", p=128), [], ["wuqn"], q="pool")
            dma(wuqr[:, :, :], w_uqr_d.rearrange("(k p) n -> p k n", p=128), [], ["wuqr"], q="pool")
            dma(wuqrs[:, :, :], w_uqrs_d.rearrange("(k p) n -> p k n", p=128), [], ["wuqrs"], q="pool")
            xt = [sb(st, "xt%d" % i, [128, 1024], F32) for i in range(3)]
            junk = sb(st, "junk", [128, 1024], BF)
            stat = sb(st, "stat", [128, 16, 4], F32)
            stat2 = sb(st, "stat2", [128, 16, 4], F32)
            xn = [sb(st, "xn%d" % i, [128, 1024], BF) for i in range(2)]
            hTg = [sb(st, "hTg%d" % i, [128, 8, 512], BF) for i in range(2)]
            hTh = [sb(st, "hTh%d" % i, [128, 8, 128], BF) for i in range(2)]
            cqn = [sb(st, "cqn%d" % i, [128, 384], BF) for i in range(2)]
            cqT = [sb(st, "cqT%d" % i, [128, 3, 512], BF) for i in range(2)]
            kwst = [sb(st, "kwst%d" % i, [128, 2, 128], BF) for i in range(2)]
            vwst = [sb(st, "vwst%d" % i, [128, 2, 128], BF) for i in range(2)]
            qst = [sb(st, "qst%d" % i, [128, 4, 512], BF) for i in range(2)]
            qrst = [sb(st, "qrst%d" % i, [128, 2, 512], BF) for i in range(2)]
            qt1 = [sb(st, "qt1_%d" % i, [128, 512], F32) for i in range(2)]
            qt2 = [sb(st, "qt2_%d" % i, [128, 512], F32) for i in range(2)]
            ropeQ = sb(st, "ropeQ", [128, 2, Nqmax], F32)
            pT = [psb(st, "pT%d" % i) for i in range(2)]
            pB = [psb(st, "pB%d" % i) for i in range(2)]
            pKa = psb(st, "pKa")
            pT2 = psb(st, "pT2")
            pQ = [psb(st, "pQ%d" % i) for i in range(2)]
            for b in range(2):
                mset("pool", vwst[b][:, :, 64:128], 1.0, [("vw1", b)])
            NXB = 6
            cqf = [sb(st, "cqf%d" % i, [128, 384], F32) for i in range(3)]
            xtb = xt + [sb(st, "xtb%d" % i, [128, 1024], F32) for i in range(NXB - len(xt))]
            ropeQl = [ropeQ, sb(st, "ropeQ1", [128, 2, seqs[1]["Nq"]], F32)]
            for q in seqs:
                dma(ropeQl[q["n"]][:, :, 0:q["Nq"]], q["ropeQ"], [], [("ropeQ", q["n"])])
            tiles = []
            for q in seqs:
                for u in range(q["NT"]):
                    own = 1 <= u <= q["NT"] - 2
                    o = u - 1
                    tiles.append(dict(i=len(tiles), q=q, n=q["n"], u=u, own=own, g=(o // 4 if own else 0), tl=(o % 4 if own else 0),
                                      hb=(0 if u == 0 else 1)))
            gcount = 0
            for T in tiles:
                if T["own"] and T["tl"] == 0:
                    gcount += 1
                T["gb"] = (gcount - 1) % 2 if T["own"] else 0
            p2 = pT2[:, :].bitcast(BF)
            qc = [0]

            def dstf(T):
                if T["own"]:
                    gb, tl = T["gb"], T["tl"]
                    return (lambda j: hTg[gb][:, j, tl * 128:(tl + 1) * 128]), (lambda j: ("hTg", gb, tl, j))
                hb = T["hb"]
                return (lambda j: hTh[hb][:, j, :]), (lambda j: ("hTh", hb, j))

            def bL(T):
                x = T["i"] % NXB
                dma(xtb[x][:, :], T["q"]["xo"][T["u"] * 128:(T["u"] + 1) * 128, :], [], [("xt", x)])

            def b1(T):
                x, s_ = T["i"] % NXB, T["i"] % 16
                act(junk[:, :], xtb[x][:, :], AF.Square, [("xt", x)], [("st", s_, 0)], accum_out=stat[:, s_, 0:1])

            def b2(T):
                rstd_ops(stat, T["i"] % 16, 1.0 / 1024, "st")

            def b3(T):
                i = T["i"]
                x, s_, b = i % NXB, i % 16, i % 2
                ts("dve", xn[b][:, :], xtb[x][:, :], stat[:, s_, 2:3], None, ALU.mult, None, [("xt", x), ("st", s_, 2)], [("xn", b)])

            def b4(T):
                b = T["i"] % 2
                pTv = pT[b][:, :].bitcast(BF)
                for j in range(8):
                    tp(pTv[:, j * 128:(j + 1) * 128], xn[b][:, j * 128:(j + 1) * 128], [("xn", b)], [("pT", b)])

            def b5(T):
                b, n = T["i"] % 2, T["n"]
                pTv = pT[b][:, :].bitcast(BF)
                dst, rdst = dstf(T)
                for j in range(8):
                    act(dst(j), pTv[:, j * 128:(j + 1) * 128], AF.Identity, [("pT", b), "a_all", "b_all"], [rdst(j)],
                        scale=a_all[:, j, n:n + 1], bias=b_all[:, j, n:n + 1])

            def b6(T):
                b = T["i"] % 2
                dst, rdst = dstf(T)
                if T["own"]:
                    for j in range(8):
                        mm(pB[b][:, 0:384], dst(j), wB[:, j, 0:384], j == 0, j == 7, [rdst(j), "wB"], [("pB", b)])
                for j in range(8):
                    mm(pKa[:, 0:128], wB[:, j, 384:512], dst(j), j == 0, j == 7, [rdst(j), "wB"], ["pKa"])
                for j in range(8):
                    mm(pKa[:, 128:256], wB[:, j, 512:640], dst(j), j == 0, j == 7, [rdst(j), "wB"], ["pKa"])
                for j in range(8):
                    mm(pKa[:, 256:384], dst(j), wB[:, j, 640:768], j == 0, j == 7, [rdst(j), "wB"], ["pKa"])

            def b7(T):
                i = T["i"]
                b, s_ = i % 2, i % 16
                cp("act", kwst[b][:, :, :], pKa[:, 0:256].rearrange("p (s t) -> p s t", s=2), ["pKa"], [("kwst", b)])
                cp("act", vwst[b][:, :, 0:64], pKa[:, 256:384].rearrange("p (k c) -> p k c", k=2), ["pKa", ("vw1", b)], [("vwst", b)])
                if T["own"]:
                    act(junk[:, 0:384], pB[b][:, 0:384], AF.Square, [("pB", b)], [("st2", s_, 0)], accum_out=stat2[:, s_, 0:1])
                    cp("act", cqf[i % 3][:, :], pB[b][:, 0:384], [("pB", b)], [("cqf", i % 3)])
                    if T["tl"] == 3:
                        q, g, gb = T["q"], T["g"], T["gb"]
                        dma(q["Hs"][:, :, g * 512:(g + 1) * 512].rearrange("j p t -> p j t"), hTg[gb][:, :, :],
                            [("hTg", gb, i_, j) for i_ in range(4) for j in range(8)], [])

            def b8(T):
                b, q, u = T["i"] % 2, T["q"], T["u"]
                if T["own"]:
                    rstd_ops(stat2, T["i"] % 16, 1.0 / 384, "st2")
                dma(q["Kw"][:, :, u * 128:(u + 1) * 128].rearrange("s p t -> p s t"), kwst[b][:, :, :], [("kwst", b)], [])
                dma(q["Vw"][:, u, :], vwst[b][:, :, :].rearrange("p k c -> p (k c)"), [("vwst", b)], [])

            def b9(T):
                if not T["own"]:
                    return
                b, s_ = T["i"] % 2, T["i"] % 16
                stt(cqn[b][:, :], cqf[T["i"] % 3][:, :], stat2[:, s_, 2:3], gq[:, :], ALU.mult, ALU.mult,
                    [("cqf", T["i"] % 3), ("st2", s_, 2), "gq"], [("cqn", b)])

            def b10(T):
                if not T["own"]:
                    return
                b = T["i"] % 2
                for c in range(3):
                    tp(p2[:, c * 128:(c + 1) * 128], cqn[b][:, c * 128:(c + 1) * 128], [("cqn", b)], ["pT2"])

            def b11(T):
                if not T["own"]:
                    return
                gb, tl = T["gb"], T["tl"]
                cp("act", cqT[gb][:, :, tl * 128:(tl + 1) * 128], p2[:, 0:384].rearrange("p (c t) -> p c t", c=3), ["pT2"], [("cqT", gb, tl)])

            def b12(T):
                if not (T["own"] and T["tl"] == 3):
                    return
                q, g, gb, n = T["q"], T["g"], T["gb"], T["n"]
                rall = [("cqT", gb, i_) for i_ in range(4)]
                for p in range(4):
                    pq, rq = pQ[qc[0] % 2], ("pQ", qc[0] % 2)
                    qc[0] += 1
                    for c in range(3):
                        mm(pq[:, :], wuqn[:, c, p * 128:(p + 1) * 128], cqT[gb][:, c, :], c == 0, c == 2, rall + ["wuqn"], [rq])
                    cp("act" if p % 2 == 0 else "dve", qst[gb][:, p, :], pq[:, :], [rq], [("qst", gb, p)])
                    for hh in range(2):
                        dma(q["Qs"][2 * p + hh, 0:64, g * 512:(g + 1) * 512], qst[gb][hh * 64:(hh + 1) * 64, p, :], [("qst", gb, p)], [])
                for rr in range(2):
                    pa, ra = pQ[qc[0] % 2], ("pQ", qc[0] % 2)
                    qc[0] += 1
                    for c in range(3):
                        mm(pa[:, :], wuqr[:, c, rr * 128:(rr + 1) * 128], cqT[gb][:, c, :], c == 0, c == 2, rall + ["wuqr"], [ra])
                    tt("dve", qt1[rr][:, :], pa[:, :], ropeQl[n][:, 0, g * 512:(g + 1) * 512], ALU.mult, [ra, ("ropeQ", n)], [("qt1", rr)])
                    pb_, rb = pQ[qc[0] % 2], ("pQ", qc[0] % 2)
                    qc[0] += 1
                    for c in range(3):
                        mm(pb_[:, :], wuqrs[:, c, rr * 128:(rr + 1) * 128], cqT[gb][:, c, :], c == 0, c == 2, rall + ["wuqrs"], [rb])
                    tt("dve", qt2[rr][:, :], pb_[:, :], ropeQl[n][:, 1, g * 512:(g + 1) * 512], ALU.mult, [rb, ("ropeQ", n)], [("qt2", rr)])
                    tt("pool", qrst[gb][:, rr, :], qt1[rr][:, :], qt2[rr][:, :], ALU.add, [("qt1", rr), ("qt2", rr)], [("qrst", gb, rr)])
                    for i_ in range(4):
                        dma(q["Qs"][4 * rr + i_, 64:96, g * 512:(g + 1) * 512], qrst[gb][i_ * 32:(i_ + 1) * 32, rr, :], [("qrst", gb, rr)], [])

            swpipe(tiles, [bL, None, b1, b2, b3, b4, b5, b6, b7, b8, b9, b10, b11, b12])
            sch.emit()
            if stop == 2:
                return nc

        with ExitStack() as st:
            KT = [sb(st, "KT%d" % i, [96, Smax], BF) for i in range(2)]
            VT = [sb(st, "VT%d" % i, [128, Smax // 128, 128], BF) for i in range(2)]
            QT = [sb(st, "QT%d" % i, [96, Nqmax], BF) for i in range(2)]
            PT = [sb(st, "PT%d" % i, [128, 1024], BF) for i in range(3)]
            rec = [sb(st, "rec%d" % i, [128, 512], F32) for i in range(2)]
            ybst = [sb(st, "ybst%d" % i, [64, 512], BF) for i in range(2)]
            ps_s = [st.enter_context(nc.psum_tensor("ps2_%d" % i, [128, 1024], F32)) for i in range(2)]
            pacc = [psb(st, "pacc%d" % i) for i in range(2)]
            NP = 4
            work = [(q, h) for q in seqs for h in range(8)]

            def loadhead(i):
                q, h = work[i]
                b = i % 2
                S, Nq = q["S"], q["Nq"]
                pc = S // NP
                dma(QT[b][0:96, 0:Nq], q["Qs"][h, :, :], [], [("QT", b)])
                for pi in range(NP):
                    dma(KT[b][0:64, pi * pc:(pi + 1) * pc], q["Ks"][h, :, pi * pc:(pi + 1) * pc], [], [("KTn", b, pi)])
                    dma(KT[b][64:96, pi * pc:(pi + 1) * pc], q["Kr"][:, pi * pc:(pi + 1) * pc], [], [("KTr", b, pi)])
                    nb = pc // 128
                    dma(VT[b][:, pi * nb:(pi + 1) * nb, :], q["Vs"][h, :, pi * nb:(pi + 1) * nb, :], [], [("VT", b, pi)])
            loadhead(0)
            scount = 0
            pcount = 0
            acount = 0
            SC = 96.0 ** -0.5
            for i, (q, h) in enumerate(work):
                if i + 1 < len(work):
                    loadhead(i + 1)
                b = i % 2
                S, Nq = q["S"], q["Nq"]
                nkb = S // 128
                kpp = nkb // NP
                for qg in range(Nq // 512):
                    a = acount % 2
                    acount += 1
                    sidx = {}
                    pidx = {}

                    npair = nkb // 2

                    def smm(p):
                        nonlocal scount
                        si = scount % 2
                        scount += 1
                        sidx[p] = si
                        for h2 in range(2):
                            kb = 2 * p + h2
                            pi = kb // kpp
                            mm(ps_s[si][:, h2 * 512:(h2 + 1) * 512], KT[b][0:96, kb * 128:(kb + 1) * 128],
                               QT[b][0:96, qg * 512:(qg + 1) * 512], True, True,
                               [("KTn", b, pi), ("KTr", b, pi), ("QT", b)], [("ps_s", si)])
                    smm(0)
                    if npair > 1:
                        smm(1)
                    for p in range(npair):
                        si = sidx[p]
                        pj = pcount % 3
                        pcount += 1
                        act(PT[pj][:, :], ps_s[si][:, :], AF.Exp, [("ps_s", si)], [("PT", pj)], scale=SC)
                        if p + 2 < npair:
                            smm(p + 2)
                        for h2 in range(2):
                            kb = 2 * p + h2
                            mm(pacc[a][:, :], VT[b][:, kb, :], PT[pj][:, h2 * 512:(h2 + 1) * 512], kb == 0, kb == nkb - 1,
                               [("VT", b, kb // kpp), ("PT", pj)], [("pacc", a)])
                    recip(rec[a][64:128, :], pacc[a][64:128, :], [("pacc", a)], [("rec", a)])
                    tt("dve", ybst[a][0:64, :], pacc[a][0:64, :], rec[a][64:128, :], ALU.mult, [("pacc", a), ("rec", a)], [("ybst", a)])
                    dma(q["Ybs"][h // 2, (h % 2) * 64:(h % 2) * 64 + 64, qg * 512:(qg + 1) * 512], ybst[a][0:64, :], [("ybst", a)], [])
            sch.emit()
            if stop == 3:
                return nc

        with ExitStack() as st:
            wD = sb(st, "wD", [128, 8, 3584], BF)
            woa = sb(st, "woa", [128, 4, 1024], BF)
            wob = sb(st, "wob", [128, 4, 1024], BF)
            wout = sb(st, "wout", [128, 8, 1024], BF)
            for pc in range(2):
                dma(wD[:, :, pc * 1792:(pc + 1) * 1792],
                    w_inD_d[:, pc * 1792:(pc + 1) * 1792].rearrange("(k p) n -> p k n", p=128), [], [("wD", pc)], q="pool")
            dma(woa[:, :, :], w_oa_d.rearrange("(k p) n -> p k n", p=128), [], ["woa"], q="pool")
            dma(wob[:, :, :], w_ob_d.rearrange("(k p) n -> p k n", p=128), [], ["wob"], q="pool")
            dma(wout[:, :, :], w_out_d.rearrange("(k p) n -> p k n", p=128), [], ["wout"], q="pool")
            rwD = [("wD", 0), ("wD", 1)]
            xr = [sb(st, "xr%d" % i, [128, 1024], F32) for i in range(2)]
            hTg = [sb(st, "hTg%d" % i, [128, 8, 512], BF) for i in range(2)]
            qaT = sb(st, "qaT", [128, 4, 512], BF)
            sz = sb(st, "sz", [128, 8, 512], BF)
            tmpF = [sb(st, "tmpF%d" % i, [128, 512], F32) for i in range(4)]
            tab = [sb(st, "tab%d" % i, [128, 512], BF) for i in range(4)]
            kw = [sb(st, "kw%d" % i, [128, 4, 768], BF) for i in range(2)]
            vw = [sb(st, "vw%d" % i, [128, 6, 256], BF) for i in range(2)]
            ybT = [sb(st, "ybT%d" % i, [128, 4, 512], BF) for i in range(2)]
            PW = [sb(st, "PW%d" % i, [128, 512], BF) for i in range(3)]
            mergedT = sb(st, "mergedT", [128, 8, 512], BF)
            res = [sb(st, "res%d" % i, [128, 1024], F32) for i in range(2)]
            yst = [sb(st, "yst0", [128, 1024], F32)] * 2
            stat = sb(st, "stat", [128, 16, 4], F32)
            wmask = sb(st, "wmask", [128, 2, 2], F32)
            PB = [psb(st, "PB%d" % i) for i in range(8)]
            for q in seqs:
                dma(wmask[:, q["n"], :], q["wmask"], [], ["wmask"])
            glist = [(q, g) for q in seqs for g in range(q["Nq"] // 512)]

            def loadgroup(i):
                q, g = glist[i]
                gb = i % 2
                dma(hTg[gb][:, :, :], q["Hs"][:, :, g * 512:(g + 1) * 512].rearrange("j p t -> p j t"), [], [("hTg", gb)])
                for v in range(4):
                    half, kvh = v % 2, v // 2
                    sel = 0 if half == kvh else 1
                    dma(kw[gb][half * 64:(half + 1) * 64, v, :], q["Kw"][sel, half * 64:(half + 1) * 64, 4 * g * 128:(4 * g + 6) * 128],
                        ["kwz"], [("kw", gb, v)])
                dma(vw[gb][:, :, :], q["Vw"][:, 4 * g:4 * g + 6, :], [], [("vw", gb)])
                dma(ybT[gb][:, :, :], q["Ybs"][:, :, g * 512:(g + 1) * 512].rearrange("c p t -> p c t"), [], [("ybT", gb)])
            for b_ in range(2):
                mset("pool", kw[b_][:, :, :], 0.0, ["kwz"])
            loadgroup(0)
            tf = 0
            wcount = 0
            acount = 0
            ocount = 0
            SCW = 64.0 ** -0.5
            for i, (q, g) in enumerate(glist):
                n, Nq, NT = q["n"], q["Nq"], q["NT"]
                gb = i % 2
                if i + 1 < len(glist):
                    loadgroup(i + 1)
                H = hTg[gb]
                rH = [("hTg", gb)]

                def inproj(col0, bank, rbank):
                    for j in range(8):
                        mm(PB[bank][:, :], wD[:, j, col0:col0 + 128], H[:, j, :], j == 0, j == 7, rH + rwD, [rbank])
                for c in range(4):
                    bk = c % 2
                    inproj(c * 128, bk, ("PB", bk))
                    cp("dve", qaT[:, c, :], PB[bk][:, :], [("PB", bk)], [("qaT", c)])
                for c in range(8):
                    bk = c % 2
                    inproj(512 + c * 128, bk, ("PB", bk))
                    t_ = tf % 4
                    tf += 1
                    act(tmpF[t_][:, :], PB[bk][:, :], AF.Tanh, [("PB", bk)], [("tmpF", t_)], scale=0.5)
                    stt(sz[:, c, :], tmpF[t_][:, :], 1.0, PB[bk][:, :], ALU.add, ALU.mult, [("tmpF", t_), ("PB", bk)], [("sz", c)])
                for c in range(4):
                    tt("pool", ybT[gb][:, c, :], ybT[gb][:, c, :], sz[:, 4 + c, :], ALU.mult, [("ybT", gb), ("sz", 4 + c)], [("ybT", gb)])
                if i == 0:
                    print("D window starts at op", Sched.nops, flush=True)
                for tl in range(4):
                    o = 4 * g + tl
                    for kvh in range(2):
                        a = 4 + (acount % 2)
                        acount += 1
                        for r in range(3):
                            wt = tl + r
                            bk = 2 + (wcount % 2)
                            pw = wcount % 3
                            wcount += 1
                            mm(PB[bk][:, :], ident_bf[:, :], wb[:, (r * 8 + kvh * 4) * 128:(r * 8 + kvh * 4 + 4) * 128],
                               True, False, ["ident", "wb"], [("PB", bk)])
                            for g4 in range(4):
                                hh = kvh * 4 + g4
                                half, chunk = hh % 2, hh // 2
                                v = kvh * 2 + half
                                mm(PB[bk][:, g4 * 128:(g4 + 1) * 128],
                                   kw[gb][:, v, wt * 128:(wt + 1) * 128],
                                   qaT[:, chunk, tl * 128:(tl + 1) * 128],
                                   False, g4 == 3, [("kw", gb, v), ("qaT", chunk)], [("PB", bk)])
                            kwargs = dict(scale=SCW)
                            if o == 0 and r == 0:
                                kwargs["bias"] = wmask[:, n, 0:1]
                            if o == Nq // 128 - 1 and r == 2:
                                kwargs["bias"] = wmask[:, n, 1:2]
                            act(PW[pw][:, :], PB[bk][:, :], AF.Exp, [("PB", bk), "wmask"], [("PW", pw)], **kwargs)
                            mm(PB[a][:, :], vw[gb][:, wt, kvh * 128:(kvh + 1) * 128], PW[pw][:, :], r == 0, r == 2,
                               [("vw", gb), ("PW", pw)], [("PB", a)])
                        t0 = tf % 4
                        t1 = (tf + 1) % 4
                        tf += 2
                        for g4 in range(4):
                            act(tmpF[t0][64:128, g4 * 128:(g4 + 1) * 128], PB[a][64:128, g4 * 128:(g4 + 1) * 128], AF.Identity,
                                [("PB", a), "es"], [("tmpF", t0)], bias=es[64:128, kvh * 4 + g4:kvh * 4 + g4 + 1])
                        recip(tmpF[t0][64:128, :], tmpF[t0][64:128, :], [("tmpF", t0)], [("tmpF", t0)])
                        tt("dve", tmpF[t1][0:64, :], PB[a][0:64, :], tmpF[t0][64:128, :], ALU.mult,
                           [("PB", a), ("tmpF", t0)], [("tmpF", t1)])
                        tt("dve", tmpF[t1][64:128, :], PB[a][0:64, :], tmpF[t0][64:128, :], ALU.mult,
                           [("PB", a), ("tmpF", t0)], [("tmpF", t1)])
                        for g4 in range(4):
                            hh = kvh * 4 + g4
                            half, chunk = hh % 2, hh // 2
                            dsl = sz[half * 64:(half + 1) * 64, chunk, tl * 128:(tl + 1) * 128]
                            tt("pool", dsl, tmpF[t1][half * 64:(half + 1) * 64, g4 * 128:(g4 + 1) * 128], dsl, ALU.mult,
                               [("tmpF", t1), ("sz", chunk)], [("sz", chunk)])
                if i == 0:
                    print("D merge starts at op", Sched.nops, flush=True)
                for oc in range(8):
                    ta = tf % 4
                    tb_ = (tf + 1) % 4
                    tf += 2
                    inproj(1536 + oc * 128, 0, ("PB", 0))
                    act(tab[ta][:, :], PB[0][:, :], AF.Tanh, [("PB", 0)], [("tab", ta)], scale=0.5)
                    inproj(1536 + 1024 + oc * 128, 1, ("PB", 1))
                    act(tab[tb_][:, :], PB[1][:, :], AF.Tanh, [("PB", 1)], [("tab", tb_)], scale=0.5)
                    for c in range(4):
                        mm(PB[2][:, :], woa[:, c, oc * 128:(oc + 1) * 128], sz[:, c, :], c == 0, c == 3, [("sz", c), "woa"], [("PB", 2)])
                    for c in range(4):
                        mm(PB[3][:, :], wob[:, c, oc * 128:(oc + 1) * 128], ybT[gb][:, c, :], c == 0, c == 3, [("ybT", gb), "wob"], [("PB", 3)])
                    stt(tmpF[ta][:, :], tab[ta][:, :], 1.0, PB[2][:, :], ALU.add, ALU.mult, [("tab", ta), ("PB", 2)], [("tmpF", ta)])
                    stt(tmpF[tb_][:, :], tab[tb_][:, :], 1.0, PB[3][:, :], ALU.add, ALU.mult, [("tab", tb_), ("PB", 3)], [("tmpF", tb_)])
                    tt("pool", mergedT[:, oc, :], tmpF[ta][:, :], tmpF[tb_][:, :], ALU.add, [("tmpF", ta), ("tmpF", tb_)], [("mT", oc)])
                def out_tail(o_, ob_, s_, q=q):
                    stt(yst[ob_][:, :], res[ob_][:, :], stat[:, s_, 2:3], gf[:, :], ALU.mult, ALU.mult,
                        [("res", ob_), ("st", s_, 2), "gf"], [("yst", 0)])
                    dma(q["y"][o_ * 128:(o_ + 1) * 128, :], yst[ob_][:, :], [("yst", 0)], [])
                pend = None
                for tl in range(4):
                    o = 4 * g + tl
                    ob = ocount % 2
                    s = ocount % 16
                    ocount += 1
                    dma(xr[ob][:, :], q["xo"][(o + 1) * 128:(o + 2) * 128, :], [], [("xr", ob)])
                    for half in range(2):
                        bk = 6 + half
                        for c in range(8):
                            mm(PB[bk][:, :], mergedT[:, c, tl * 128:(tl + 1) * 128], wout[:, c, half * 512:(half + 1) * 512],
                               c == 0, c == 7, [("mT", c), "wout"], [("PB", bk)])
                        tt("dve", res[ob][:, half * 512:(half + 1) * 512], PB[bk][:, :], gate_bc[:, n, half * 512:(half + 1) * 512],
                           ALU.mult, [("PB", bk), "gate_bc"], [("res", ob)])
                    if pend is not None:
                        out_tail(*pend)
                    tt("pool", res[ob][:, :], res[ob][:, :], xr[ob][:, :], ALU.add, [("res", ob), ("xr", ob)], [("res", ob)])
                    act(qaT[:, 0:2, :].rearrange("p c t -> p (c t)"), res[ob][:, :], AF.Square, [("res", ob)],
                        [("st", s, 0), ("qaT", 0), ("qaT", 1)], accum_out=stat[:, s, 0:1])
                    rstd_ops(stat, s, 1.0 / 1024, "st")
                    pend = (o, ob, s)
                out_tail(*pend)
            sch.emit()
            if stop == 4:
                return nc
    return nc


def _rope_tables(S):
    inv = (1.0 / (np.float32(10000.0) ** (np.arange(0, 32, 2, dtype=np.float32) / np.float32(32)))).astype(np.float32)
    ang = (np.arange(S, dtype=np.float32)[:, None] * inv[None, :]).astype(np.float32)
    cos = np.cos(ang).astype(np.float32)
    sin = np.sin(ang).astype(np.float32)
    C = np.concatenate([cos, cos], 1)
    Sn = np.concatenate([-sin, sin], 1)
    return C, Sn


def _wbias():
    slopes = (2.0 ** (-8.0 / 8)) ** np.arange(1, 9, dtype=np.float64)
    out = np.zeros((128, 3, 8, 128), np.float32)
    j = np.arange(128)[:, None]
    i = np.arange(128)[None, :]
    for r in range(3):
        rel = (r * 128 + j) - 128 - i
        valid = np.abs(rel) <= 128
        for h in range(8):
            b = -slopes[h] * np.abs(rel) * 8.0
            out[:, r, h, :] = np.where(valid, b, -240000.0)
    return out.reshape(128, 3072)


def _prep(inputs, S_s, S_p):
    f = lambda a: np.ascontiguousarray(np.asarray(a, dtype=np.float32))
    x_s, x_p = f(inputs["x_sample"]), f(inputs["x_prompt"])
    c_s, c_p = f(inputs["c_sample"]), f(inputs["c_prompt"])
    w_in = f(inputs["w_in"])[0]
    w_ukv = f(inputs["w_ukv"])[0].reshape(256, 8, 128)
    w_uq = f(inputs["w_uq"])[0].reshape(384, 8, 96)
    qa, ka, va, za = w_in[:, 0:512], w_in[:, 512:640], w_in[:, 640:768], w_in[:, 768:1280]
    cq, ckv, kr = w_in[:, 1280:1664], w_in[:, 1664:1920], w_in[:, 1920:1952]
    zb, gm = w_in[:, 1952:2464], w_in[:, 2464:4512]
    krsw = np.concatenate([kr[:, 16:32], kr[:, 0:16]], 1)
    kasw = np.concatenate([ka[:, 64:128], ka[:, 0:64]], 1)
    uqr = w_uq[:, :, 64:96]
    uqrs = np.concatenate([uqr[:, :, 16:32], uqr[:, :, 0:16]], 2)
    shared = {
        "w_ada": f(inputs["w_ada"])[0],
        "b_adaT": f(f(inputs["b_ada"])[0].reshape(24, 128).T),
        "b_gate": f(f(inputs["b_ada"])[0][2048:3072].reshape(1, 1024)),
        "b_scale": f(f(inputs["b_ada"])[0][1024:2048].reshape(1, 1024)),
        "gn_bc": f(np.broadcast_to(f(inputs["g_norm"])[0][None, :], (128, 1024))),
        "g_normT": f(f(inputs["g_norm"])[0].reshape(8, 128).T),
        "w_inA": f(np.concatenate([ckv, kr, krsw], 1)),
        "w_inB": f(np.concatenate([cq, ka, kasw, va], 1)),
        "w_inD": f(np.concatenate([qa, za, zb, gm], 1)),
        "gkv_bc": f(np.broadcast_to(f(inputs["g_kv"])[0][None, :], (128, 256))),
        "gq_bc": f(np.broadcast_to(f(inputs["g_q"])[0][None, :], (128, 384))),
        "gf_bc": f(np.broadcast_to(f(inputs["g_final"])[None, :], (128, 1024))),
        "w_uk": f(w_ukv[:, :, 0:64].reshape(256, 512)),
        "w_uv": f(w_ukv[:, :, 64:128].reshape(256, 512)),
        "w_uqn": f(w_uq[:, :, 0:64].reshape(384, 512)),
        "w_uqr": f(uqr.reshape(384, 256)),
        "w_uqrs": f(uqrs.reshape(384, 256)),
        "sink_bc": f(np.broadcast_to(f(inputs["sink"])[0][None, :], (128, 8))),
        "w_oa": f(inputs["w_oa"])[0],
        "w_ob": f(inputs["w_ob"])[0],
        "w_out": f(inputs["w_out"])[0],
        "wbias": _wbias(),
        "ident": np.eye(128, dtype=np.float32),
    }
    tabs = {}
    for nm, S in (("s", S_s), ("p", S_p)):
        C, Sn = _rope_tables(S)
        tabs[nm] = (C, Sn, f(np.concatenate([C, Sn], 1)))
    maps = []
    for c in range(NCORES):
        m = dict(shared)
        cs = [c_s[c // 4], c_p[c // 2]]
        m["cT"] = f(np.stack([v.reshape(8, 128).T for v in cs], -1))
        for nm, S, x, bidx, part, nparts in (("s", S_s, x_s, c // 4, c % 4, 4), ("p", S_p, x_p, c // 2, c % 2, 2)):
            Nq = S // nparts
            q0 = part * Nq
            C, Sn, CS = tabs[nm]
            xs = x[bidx]
            m["xf_" + nm] = xs
            xo = np.zeros((Nq + 256, 1024), np.float32)
            lo, hi = max(q0 - 128, 0), min(q0 + Nq + 128, S)
            xo[lo - (q0 - 128):hi - (q0 - 128)] = xs[lo:hi]
            m["xo_" + nm] = xo
            m["ropeK_" + nm] = CS
            rq = np.stack([np.tile(C[q0:q0 + Nq].T, (4, 1)), np.tile(Sn[q0:q0 + Nq].T, (4, 1))], 1)
            m["ropeQ_" + nm] = f(rq)
            wm = np.zeros((128, 2), np.float32)
            if q0 == 0:
                wm[:, 0] = -30000.0
            if q0 + Nq == S:
                wm[:, 1] = -30000.0
            m["wmask_" + nm] = wm
        maps.append(m)
    return maps


_NC_CACHE = {}


def kernel(**inputs):
    S_s = int(np.shape(inputs["x_sample"])[1])
    S_p = int(np.shape(inputs["x_prompt"])[1])
    maps = _prep(inputs, S_s, S_p)
    import os
    if os.environ.get("K_MAXOPS"):
        Sched.maxops = int(os.environ["K_MAXOPS"])
    stop = os.environ.get("K_STOP")
    stop = int(stop) if stop else None
    key = (S_s, S_p, stop)
    if key not in _NC_CACHE:
        _NC_CACHE[key] = build(S_s, S_p, stop)
    nc = _NC_CACHE[key]
    ncr = int(os.environ.get("K_NCORES", NCORES))
    print("total ops recorded", Sched.nops, flush=True)
    c0 = int(os.environ.get('K_C0', 0))
    res = run_bass_kernel_spmd(nc, maps[c0:c0 + ncr], core_ids=list(range(ncr)))
    if ncr < NCORES:
        return res
    B_p = np.shape(inputs["x_prompt"])[0]
    B_s = np.shape(inputs["x_sample"])[0]
    y_p = np.zeros((B_p, S_p, 1024), np.float32)
    y_s = np.zeros((B_s, S_s, 1024), np.float32)
    for c in range(NCORES):
        r = res.results[c]
        Nq = S_s // 4
        y_s[c // 4, (c % 4) * Nq:(c % 4 + 1) * Nq] = np.asarray(r["y_s"], dtype=np.float32)
        Nq = S_p // 2
        y_p[c // 2, (c % 2) * Nq:(c % 2 + 1) * Nq] = np.asarray(r["y_p"], dtype=np.float32)
    return (y_p, y_s)
```

```python
import numpy as np
from contextlib import ExitStack
import concourse.bass as bass
import concourse.mybir as mybir
from concourse.bass_utils import run_bass_kernel_spmd

F32 = mybir.dt.float32
BF = mybir.dt.bfloat16
AF = mybir.ActivationFunctionType
ALU = mybir.AluOpType
EPS = 1e-6
NCORES = 8
import os as _os
EV_SPLIT = int(_os.environ.get('K_EVSPLIT', 8))
ENGS = ("pe", "act", "dve", "pool", "sp")


class _Op:
    __slots__ = ("eng", "fn", "deps", "dma", "sem", "val", "signal", "phase")


class Sched:
    def __init__(self, nc, esems, dsems):
        self.nc = nc
        self.esem = esems
        self.dsems = dsems
        self.ndma = {q: 0 for q in dsems}
        self.dma_last = {q: [None] * len(dsems[q]) for q in dsems}
        self.state = {}
        self.ops = []
        self.cnt = {e: 0 for e in ENGS}
        self.known = {e: {} for e in ENGS}
        self.phase = 0

    maxops = None
    nops = 0

    def add(self, eng, fn, r=(), w=(), dma=False):
        Sched.nops += 1
        if Sched.maxops is not None and Sched.nops > Sched.maxops:
            return None
        op = _Op()
        op.eng, op.fn, op.dma, op.signal, op.phase = eng, fn, dma, False, self.phase
        op.sem = None
        op.val = 0
        deps = {}
        for x in r:
            st = self.state.get(x)
            if st is not None and st[0] is not None:
                deps[id(st[0])] = st[0]
        for x in w:
            st = self.state.get(x)
            if st is not None:
                if st[0] is not None:
                    deps[id(st[0])] = st[0]
                for o in st[1].values():
                    deps[id(o)] = o
                for o in st[2]:
                    deps[id(o)] = o
        if dma:
            pool_ = self.dsems[eng]
            slot = self.ndma[eng] % len(pool_)
            prev = self.dma_last[eng][slot]
            if prev is not None:
                deps[id(prev)] = prev
            op.sem = pool_[slot]
            op.val = (self.ndma[eng] // len(pool_) + 1) * 16
            self.dma_last[eng][slot] = op
            self.ndma[eng] += 1
        op.deps = [d for d in deps.values()
                   if d.phase == self.phase and not (not d.dma and not dma and d.eng == eng == "pe")]
        for d in op.deps:
            d.signal = True
        for x in r:
            st = self.state.setdefault(x, [None, {}, []])
            if dma:
                st[2].append(op)
            else:
                st[1][eng] = op
        for x in w:
            self.state[x] = [op, {}, []]
        self.ops.append(op)
        return op

    def emit(self):
        ops = self.ops
        for op in ops:
            if not op.dma and op.signal:
                self.cnt[op.eng] += 1
                op.sem = self.esem[op.eng]
                op.val = self.cnt[op.eng]
        per = {e: [] for e in ENGS}
        for op in ops:
            per[op.eng].append(op)
        phase = self.phase

        def run(eng, e):
            known = self.known[eng]
            for op in per[eng]:
                for d in op.deps:
                    if known.get(d.sem, 0) < d.val:
                        e.wait_ge(d.sem, d.val)
                        known[d.sem] = d.val
                ins = op.fn(e)
                if op.dma:
                    ins.then_inc(op.sem, 16)
                elif op.signal:
                    ins.then_inc(op.sem, 1)
            if eng == "sp":
                for lst in self.dma_last.values():
                    for last in lst:
                        if last is not None and last.phase == phase and known.get(last.sem, 0) < last.val:
                            e.wait_ge(last.sem, last.val)
                            known[last.sem] = last.val

        with self.nc.Block() as block:
            @block.tensor
            def _(e):
                run("pe", e)

            @block.scalar
            def _(e):
                run("act", e)

            @block.vector
            def _(e):
                run("dve", e)

            @block.gpsimd
            def _(e):
                run("pool", e)

            @block.sync
            def _(e):
                run("sp", e)
        self.phase += 1
        self.ops = []


def build(S_s, S_p, stop=None):
    nc = bass.Bass("TRN2", target_bir_lowering=False)
    seqs = [dict(n=0, nm="s", S=S_s, Nq=S_s // 4), dict(n=1, nm="p", S=S_p, Nq=S_p // 2)]
    Smax = max(S_s, S_p)
    Nqmax = max(q["Nq"] for q in seqs)

    def din(name, shape, dt=F32):
        return nc.dram_tensor(name, list(shape), dt, kind="ExternalInput").ap()

    def dscr(name, shape, dt=BF):
        if _os.environ.get("K_DBG") and name[:2] in ("Ks", "Kr", "Vs"):
            return nc.dram_tensor(name, list(shape), dt, kind="ExternalOutput").ap()
        return nc.dram_tensor(name, list(shape), dt).ap()

    for q in seqs:
        nm, S, Nq = q["nm"], q["S"], q["Nq"]
        NT = Nq // 128 + 2
        q["NT"] = NT
        q["xf"] = din("xf_" + nm, [S, 1024])
        q["xo"] = din("xo_" + nm, [Nq + 256, 1024])
        q["ropeK"] = din("ropeK_" + nm, [S, 64])
        q["ropeQ"] = din("ropeQ_" + nm, [128, 2, Nq])
        q["wmask"] = din("wmask_" + nm, [128, 2])
        q["y"] = nc.dram_tensor("y_" + nm, [Nq, 1024], F32, kind="ExternalOutput").ap()
        q["Ks"] = dscr("Ks_" + nm, [8, 64, S])
        q["Kr"] = dscr("Kr_" + nm, [32, S])
        q["Vs"] = dscr("Vs_" + nm, [8, 128, S // 128, 128])
        q["Qs"] = dscr("Qs_" + nm, [8, 96, Nq])
        q["Kw"] = dscr("Kw_" + nm, [2, 128, NT * 128])
        q["Vw"] = dscr("Vw_" + nm, [128, NT, 256])
        q["Ybs"] = dscr("Ybs_" + nm, [4, 128, Nq])
        q["Hs"] = dscr("Hs_" + nm, [8, 128, Nq])
    cT_d = din("cT", [128, 8, 2])
    w_ada_d = din("w_ada", [1024, 3072])
    b_adaT_d = din("b_adaT", [128, 24])
    b_gate_d = din("b_gate", [1, 1024])
    b_scale_d = din("b_scale", [1, 1024])
    gn_bc_d = din("gn_bc", [128, 1024])
    g_normT_d = din("g_normT", [128, 8])
    w_inA_d = din("w_inA", [1024, 320])
    w_inB_d = din("w_inB", [1024, 768])
    w_inD_d = din("w_inD", [1024, 3584])
    gkv_d = din("gkv_bc", [128, 256])
    gq_d = din("gq_bc", [128, 384])
    gf_d = din("gf_bc", [128, 1024])
    w_uk_d = din("w_uk", [256, 512])
    w_uv_d = din("w_uv", [256, 512])
    w_uqn_d = din("w_uqn", [384, 512])
    w_uqr_d = din("w_uqr", [384, 256])
    w_uqrs_d = din("w_uqrs", [384, 256])
    sink_d = din("sink_bc", [128, 8])
    w_oa_d = din("w_oa", [512, 1024])
    w_ob_d = din("w_ob", [512, 1024])
    w_out_d = din("w_out", [1024, 1024])
    wbias_d = din("wbias", [128, 3072])
    ident_d = din("ident", [128, 128])

    with ExitStack() as top:
        uid = [0]

        def sb(st, name, shape, dt):
            uid[0] += 1
            return st.enter_context(nc.sbuf_tensor("%s_%d" % (name, uid[0]), list(shape), dt))

        def psb(st, name):
            uid[0] += 1
            return st.enter_context(nc.psum_tensor("%s_%d" % (name, uid[0]), [128, 512], F32))

        esems = {e: top.enter_context(nc.semaphore("sem_" + e)) for e in ENGS}
        dsems = {"sp": [top.enter_context(nc.semaphore("dsem%d" % i)) for i in range(int(_os.environ.get("K_NSEM", 12)))],
                 "pool": [top.enter_context(nc.semaphore("psem%d" % i)) for i in range(4)]}
        sch = Sched(nc, esems, dsems)
        A = sch.add

        def mm(out, lhsT, rhs, start, stop, r, w):
            return A("pe", lambda e: e.matmul(out, lhsT, rhs, start=start, stop=stop), r, w)

        def tp(out, in_, r, w):
            return A("pe", lambda e: e.transpose(out, in_, ident_bf[:, :]), r, w)

        def act(out, in_, func, r, w, **kw):
            return A("act", lambda e: e.activation(out, in_, func, **kw), r, w)

        def ts(eng, out, in0, s1, s2, op0, op1, r, w):
            if op1 is None:
                return A(eng, lambda e: e.tensor_scalar(out=out, in0=in0, scalar1=s1, scalar2=None, op0=op0), r, w)
            return A(eng, lambda e: e.tensor_scalar(out=out, in0=in0, scalar1=s1, scalar2=s2, op0=op0, op1=op1), r, w)

        def tt(eng, out, in0, in1, op, r, w):
            return A(eng, lambda e: e.tensor_tensor(out=out, in0=in0, in1=in1, op=op), r, w)

        def stt(out, in0, scalar, in1, op0, op1, r, w):
            return A("dve", lambda e: e.scalar_tensor_tensor(out=out, in0=in0, scalar=scalar, in1=in1, op0=op0, op1=op1), r, w)

        def cp(eng, out, in_, r, w):
            if eng == "act":
                return A("act", lambda e: e.copy(out, in_), r, w)
            return A(eng, lambda e: e.tensor_copy(out=out, in_=in_), r, w)

        def recip(out, in_, r, w):
            return A("dve", lambda e: e.reciprocal(out=out, in_=in_), r, w)

        def mset(eng, ap, val, w):
            return A(eng, lambda e: e.memset(ap, val), (), w)

        def dma(out, in_, r, w, q="sp"):
            return A(q, lambda e: e.dma_start(out=out, in_=in_), r, w, dma=True)

        def rstd_ops(stat, s, inv_n, key):
            ts("pool", stat[:, s, 1:2], stat[:, s, 0:1], inv_n, EPS, ALU.mult, ALU.add, [(key, s, 0)], [(key, s, 1)])
            tt("pool", stat[:, s, 2:3], stat[:, s, 1:2], mhalf[:, 0:1], ALU.pow, [(key, s, 1)], [(key, s, 2)])

        ident_bf = sb(top, "ident_bf", [128, 128], BF)
        a_all = sb(top, "a_all", [128, 8, 2], F32)
        b_all = sb(top, "b_all", [128, 8, 2], F32)
        gate_bc = sb(top, "gate_bc", [128, 2, 1024], F32)
        gf = sb(top, "gf", [128, 1024], F32)
        wb = sb(top, "wb", [128, 3072], BF)
        es = sb(top, "es", [128, 8], F32)
        mhalf = sb(top, "mhalf", [128, 8], F32)
        gkv = sb(top, "gkv", [128, 256], F32)
        gq = sb(top, "gq", [128, 384], F32)
        b_bf = sb(top, "b_bf", [128, 8, 2], BF)
        ones_row = sb(top, "ones_row", [1, 512], BF)
        stAB = top.enter_context(ExitStack())
        a_bc = sb(stAB, "a_bc", [128, 2, 1024], F32)

        with ExitStack() as st:
            wa = [sb(st, "wa%d" % i, [128, 8, 1024], F32) for i in range(2)]
            cT = sb(st, "cTt", [128, 16], F32)
            ch = sb(st, "ch", [128, 16], F32)
            th = sb(st, "th", [128, 16], F32)
            sc = sb(st, "sc", [128, 16], F32)
            sc_rep = sb(st, "sc_rep", [128, 16, 128], F32)
            badaT = sb(st, "badaT", [128, 24], F32)
            bgate = sb(st, "bgate", [1, 1024], F32)
            gnT = sb(st, "gnT", [128, 8], F32)
            sink_t = sb(st, "sink_t", [128, 8], F32)
            ones_f = sb(st, "ones_f", [1, 128], F32)
            bscale = sb(st, "bscale", [1, 1024], F32)
            gn_bc = sb(st, "gn_bc", [128, 1024], F32)
            mod_sb = sb(st, "mod_sb", [128, 16, 2], F32)
            ps_mod = psb(st, "ps_mod")
            ps_g = [[psb(st, "ps_g%d%d" % (n, h)) for h in range(2)] for n in range(2)]

            dma(ident_bf[:, :], ident_d, [], ["ident"], q="pool")
            dma(wb[:, :], wbias_d, [], ["wb"], q="pool")
            dma(cT[:, :], cT_d.rearrange("p j n -> p (j n)"), [], ["cT"])
            dma(badaT[:, :], b_adaT_d, [], ["badaT"])
            dma(bgate[:, :], b_gate_d, [], ["bgate"])
            dma(gnT[:, :], g_normT_d, [], ["gnT"])
            dma(bscale[:, :], b_scale_d, [], ["bscale"])
            dma(gn_bc[:, :], gn_bc_d, [], ["gn_bc"])
            mset("pool", ones_row[:, :], 1.0, ["ones_row"])
            dma(sink_t[:, :], sink_d, [], ["sink"])
            dma(gf[:, :], gf_d, [], ["gf"])
            dma(gkv[:, :], gkv_d, [], ["gkv"])
            dma(gq[:, :], gq_d, [], ["gq"])
            mset("pool", mhalf[:, :], -0.5, ["mhalf"])
            mset("pool", ones_f[:, :], 1.0, ["ones_f"])
            def ldwa(third):
                dma(wa[third % 2][:, :, :],
                    w_ada_d[:, third * 1024:(third + 1) * 1024].rearrange("(k p) n -> p k n", p=128),
                    [], [("wa", third % 2)])
            ldwa(0)
            ldwa(1)
            ts("dve", ch[:, :], cT[:, :], 0.5, None, ALU.mult, None, ["cT"], ["ch"])
            act(th[:, :], ch[:, :], AF.Tanh, ["ch"], ["th"])
            stt(sc[:, :], th[:, :], 1.0, ch[:, :], ALU.add, ALU.mult, ["th", "ch"], ["sc"])
            cp("dve", sc_rep[:, :, :], sc[:, :].unsqueeze(2).broadcast_to([128, 16, 128]), ["sc"], ["sc_rep"])
            act(es[:, :], sink_t[:, :], AF.Exp, ["sink"], ["es"])
            pm = ps_mod[:, 0:32].rearrange("p (a n) -> p a n", n=2)
            for third in range(2):
                for j in range(8):
                    for k in range(8):
                        mm(pm[:, third * 8 + j, :], wa[third][:, k, j * 128:(j + 1) * 128], sc[:, 2 * k:2 * k + 2],
                           k == 0, k == 7, [("wa", third), "sc"], ["ps_mod"])
                if third == 0:
                    ldwa(2)
            for n in range(2):
                for h in range(2):
                    for k in range(8):
                        mm(ps_g[n][h][:, :], sc_rep[:, 2 * k + n, :], wa[0][:, k, h * 512:(h + 1) * 512],
                           k == 0, False, ["sc_rep", ("wa", 0)], [("ps_g", n, h)])
                    mm(ps_g[n][h][:, :], ones_f[0:1, :], bgate[0:1, h * 512:(h + 1) * 512], False, True,
                       ["ones_f", "bgate"], [("ps_g", n, h)])
                    ts("dve", gate_bc[:, n, h * 512:(h + 1) * 512], ps_g[n][h][:, :], 0.25, None, ALU.mult, None,
                       [("ps_g", n, h)], ["gate_bc"])
            for n in range(2):
                for h in range(2):
                    for k in range(8):
                        mm(ps_g[n][h][:, :], sc_rep[:, 2 * k + n, :], wa[1][:, k, h * 512:(h + 1) * 512],
                           k == 0, False, ["sc_rep", ("wa", 1)], [("ps_g", n, h)])
                    mm(ps_g[n][h][:, :], ones_f[0:1, :], bscale[0:1, h * 512:(h + 1) * 512], False, True,
                       ["ones_f", "bscale"], [("ps_g", n, h)])
                    stt(a_bc[:, n, h * 512:(h + 1) * 512], ps_g[n][h][:, :], 1.0, gn_bc[:, h * 512:(h + 1) * 512], ALU.add, ALU.mult,
                        [("ps_g", n, h), "gn_bc"], ["a_bc"])
            tt("dve", mod_sb[:, :, :], pm[:, 0:16, :], badaT[:, 0:16].unsqueeze(2).broadcast_to([128, 16, 2]), ALU.add,
               ["ps_mod", "badaT"], ["mod_sb"])
            cp("dve", b_all[:, :, :], mod_sb[:, 0:8, :], ["mod_sb"], ["b_all"])
            cp("dve", b_bf[:, :, :], mod_sb[:, 0:8, :], ["mod_sb"], ["b_bf"])
            stt(a_all[:, :, :], mod_sb[:, 8:16, :], 1.0, gnT[:, :].unsqueeze(2).broadcast_to([128, 8, 2]), ALU.add, ALU.mult,
                ["mod_sb", "gnT"], ["a_all"])
            sch.emit()
            if stop == 0:
                return nc

        def front(n, X, rx, stat, s, xn, rxn, pT, rpT, junk, dst, rdst):
            act(junk[:, :], X, AF.Square, [rx], [("st", s, 0)], accum_out=stat[:, s, 0:1])
            rstd_ops(stat, s, 1.0 / 1024, "st")
            ts("dve", xn[:, :], X, stat[:, s, 2:3], None, ALU.mult, None, [rx, ("st", s, 2)], [rxn])
            pTv = pT[:, :].bitcast(BF)
            for j in range(8):
                tp(pTv[:, j * 128:(j + 1) * 128], xn[:, j * 128:(j + 1) * 128], [rxn], [rpT])
            for j in range(8):
                if j < EV_SPLIT:
                    act(dst(j), pTv[:, j * 128:(j + 1) * 128], AF.Identity, [rpT, "a_all", "b_all"], [rdst(j)],
                        scale=a_all[:, j, n:n + 1], bias=b_all[:, j, n:n + 1])
                else:
                    ts("dve", dst(j), pTv[:, j * 128:(j + 1) * 128], a_all[:, j, n:n + 1], b_all[:, j, n:n + 1],
                       ALU.mult, ALU.add, [rpT, "a_all", "b_all"], [rdst(j)])

        def swpipe(items, stages):
            n_, m_ = len(items), len(stages)
            if _os.environ.get("K_NOSKEW"):
                for it in items:
                    for f_ in stages:
                        if f_ is not None:
                            f_(it)
                return
            for k in range(n_ + m_ - 1):
                for s_ in range(m_ - 1, -1, -1):
                    i_ = k - s_
                    if 0 <= i_ < n_ and stages[s_] is not None:
                        stages[s_](items[i_])

        with ExitStack() as st:
            wA = sb(st, "wA", [128, 8, 320], BF)
            wuk = sb(st, "wuk", [128, 2, 512], BF)
            wuv = sb(st, "wuv", [128, 2, 512], BF)
            dma(wA[:, :, :], w_inA_d.rearrange("(k p) n -> p k n", p=128), [], ["wA"], q="pool")
            dma(wuk[:, :, :], w_uk_d.rearrange("(k p) n -> p k n", p=128), [], ["wuk"], q="pool")
            dma(wuv[:, :, :], w_uv_d.rearrange("(k p) n -> p k n", p=128), [], ["wuv"], q="pool")
            NX = 6
            xt = [sb(st, "xt%d" % i, [128, 1024], F32) for i in range(NX)]
            junk = sb(st, "junk", [128, 1024], BF)
            junkb = sb(st, "junkb", [128, 384], BF)
            stat = sb(st, "stat", [128, 16, 4], F32)
            stat2 = sb(st, "stat2", [128, 16, 4], F32)
            xn = [sb(st, "xn%d" % i, [128, 1024], BF) for i in range(2)]
            hT = [sb(st, "hT%d" % i, [128, 8, 128], BF) for i in range(2)]
            ckvf = [sb(st, "ckvf%d" % i, [128, 256], F32) for i in range(3)]
            ckvn = [sb(st, "ckvn%d" % i, [128, 288], BF) for i in range(2)]
            r1 = [sb(st, "r1_%d" % i, [128, 32], F32) for i in range(4)]
            r2 = [sb(st, "r2_%d" % i, [128, 32], F32) for i in range(4)]
            ckvT = [sb(st, "ckvT%d" % i, [128, 2, 512], BF) for i in range(2)]
            krT = [sb(st, "krT%d" % i, [32, 512], BF) for i in range(2)]
            kst = [sb(st, "kst%d" % i, [128, 4, 512], BF) for i in range(2)]
            vst = [sb(st, "vst%d" % i, [128, 8, 4, 128], BF) for i in range(2)]
            ropeK = [sb(st, "ropeK%d" % i, [128, q["S"] // 128, 64], F32) for i, q in enumerate(seqs)]
            bA = sb(st, "bA", [1, 2, 320], BF)
            pT = [psb(st, "pT%d" % i) for i in range(2)]
            pL = [psb(st, "pL%d" % i) for i in range(2)]
            pT2l = [psb(st, "pT2_%d" % i) for i in range(2)]
            pK = [psb(st, "pK%d" % i) for i in range(1)] * 2
            pV = psb(st, "pV")
            for b in range(2):
                for tl in range(4):
                    mset("pool", vst[b][:, :, tl, 64:128], 1.0, [("vst1", b)])
            for n in range(2):
                for j in range(8):
                    mm(pK[0][0:1, 0:320], b_bf[:, j, n:n + 1], wA[:, j, :], j == 0, j == 7, ["b_bf", "wA"], [("pK", 0)])
                cp("act", bA[0:1, n, :], pK[0][0:1, 0:320], [("pK", 0)], ["bA"])
            for q in seqs:
                dma(ropeK[q["n"]][:, :, :], q["ropeK"].rearrange("(t p) c -> p t c", p=128), [], [("ropeK", q["n"])])
            tiles = []
            for q in seqs:
                for t in range(q["S"] // 128):
                    i = len(tiles)
                    tiles.append(dict(i=i, q=q, n=q["n"], t=t, g=t // 4, tl=t % 4, gb=(i // 4) % 2))
            p2l = [pT2l[0][:, :].bitcast(BF), pT2l[1][:, :].bitcast(BF)]

            def sL(T):
                x = T["i"] % NX
                dma(xt[x][:, :], T["q"]["xf"][T["t"] * 128:(T["t"] + 1) * 128, :], [], [("xt", x)])

            def s1(T):
                x, s_ = T["i"] % NX, T["i"] % 16
                act(junk[:, :], xt[x][:, :], AF.Square, [("xt", x)], [("st", s_, 0), "junk"], accum_out=stat[:, s_, 0:1])

            def s2(T):
                rstd_ops(stat, T["i"] % 16, 1.0 / 1024, "st")

            def s3(T):
                i, n = T["i"], T["n"]
                x, s_, b = i % NX, i % 16, i % 2
                stt(xn[b][:, :], xt[x][:, :], stat[:, s_, 2:3], a_bc[:, n, :], ALU.mult, ALU.mult,
                    [("xt", x), ("st", s_, 2), "a_bc"], [("xn", b)])

            def s4(T):
                b = T["i"] % 2
                pTv = pT[b][:, :].bitcast(BF)
                for j in range(8):
                    tp(pTv[:, j * 128:(j + 1) * 128], xn[b][:, j * 128:(j + 1) * 128], [("xn", b)], [("pT", b)])

            def s5(T):
                b = T["i"] % 2
                cp("act", hT[b][:, :, :].rearrange("p j t -> p (j t)"), pT[b][:, :].bitcast(BF), [("pT", b)], [("hT", b)])

            def s6(T):
                b, n, bl = T["i"] % 2, T["n"], T["i"] % 2
                for j in range(8):
                    mm(pL[bl][:, 0:320], hT[b][:, j, :], wA[:, j, :], j == 0, False, [("hT", b), "wA"], [("pL", bl)])
                mm(pL[bl][:, 0:320], ones_row[0:1, 0:128], bA[0:1, n, :], False, True, ["ones_row", "bA"], [("pL", bl)])

            def s7(T):
                i = T["i"]
                b, s_ = i % 2, i % 16
                act(junkb[:, 0:256], pL[b][:, 0:256], AF.Square, [("pL", b)], [("st2", s_, 0), ("pLq", b), "junkb"], accum_out=stat2[:, s_, 0:1])

            def s8(T):
                i, n, t = T["i"], T["n"], T["t"]
                b, c3, c4 = i % 2, i % 3, i % 4
                rstd_ops(stat2, i % 16, 1.0 / 256, "st2")
                rr = [("pL", b), ("pLq", b), ("ropeK", n)]
                tt("dve", r1[c4][:, :], pL[b][:, 256:288], ropeK[n][:, t, 0:32], ALU.mult, rr, [("r1", c4)])
                tt("dve", r2[c4][:, :], pL[b][:, 288:320], ropeK[n][:, t, 32:64], ALU.mult, rr, [("r2", c4)])
                cp("dve", ckvf[c3][:, :], pL[b][:, 0:256], rr, [("ckvf", c3)])

            def s9(T):
                i = T["i"]
                b, s_, c3, c4 = i % 2, i % 16, i % 3, i % 4
                stt(ckvn[b][:, 0:256], ckvf[c3][:, :], stat2[:, s_, 2:3], gkv[:, :], ALU.mult, ALU.mult,
                    [("ckvf", c3), ("st2", s_, 2), "gkv"], [("ckvn", b, 0)])
                tt("pool", ckvn[b][:, 256:288], r1[c4][:, :], r2[c4][:, :], ALU.add, [("r1", c4), ("r2", c4)], [("ckvn", b, 1)])

            def s10(T):
                b = T["i"] % 2
                p2 = p2l[b]
                rck = [("ckvn", b, 0), ("ckvn", b, 1)]
                tp(p2[:, 0:128], ckvn[b][:, 0:128], rck, [("pT2", b)])
                tp(p2[:, 128:256], ckvn[b][:, 128:256], rck, [("pT2", b)])
                tp(p2[0:32, 256:384], ckvn[b][:, 256:288], rck, [("pT2", b)])

            def s11(T):
                b, gb, tl = T["i"] % 2, T["gb"], T["tl"]
                p2 = p2l[b]
                cp("act", ckvT[gb][:, :, tl * 128:(tl + 1) * 128],
                   p2[:, 0:256].rearrange("p (c t) -> p c t", c=2), [("pT2", b)], [("ckvT", gb, tl)])
                cp("act", krT[gb][0:32, tl * 128:(tl + 1) * 128], p2[0:32, 256:384], [("pT2", b)], [("krT", gb, tl)])

            def s12(T):
                gb, tl = T["gb"], T["tl"]
                for c in range(2):
                    mm(pV[:, :], ckvT[gb][:, c, tl * 128:(tl + 1) * 128], wuv[:, c, :], c == 0, c == 1,
                       [("ckvT", gb, tl), "wuv"], ["pV"])

            def s13(T):
                gb, tl = T["gb"], T["tl"]
                cp("dve", vst[gb][:, :, tl, 0:64], pV[:, :].rearrange("p (h c) -> p h c", h=8),
                   ["pV", ("vst1", gb)], [("vst", gb, tl)])

            def s14(T):
                if T["tl"] != 3:
                    return
                gb = T["gb"]
                rall = [("ckvT", gb, i) for i in range(4)]
                for p in range(4):
                    for c in range(2):
                        mm(pK[0][:, :], wuk[:, c, p * 128:(p + 1) * 128], ckvT[gb][:, c, :], c == 0, c == 1,
                           rall + ["wuk"], [("pK", 0)])
                    cp("act" if p % 2 == 0 else "dve", kst[gb][:, p, :], pK[0][:, :], [("pK", 0)], [("kst", gb, p)])

            def s15(T):
                if T["tl"] != 3:
                    return
                q, g, gb = T["q"], T["g"], T["gb"]
                dma(q["Ks"].rearrange("h d t -> (h d) t")[:, g * 512:(g + 1) * 512].rearrange("(p r) t -> r p t", r=128),
                    kst[gb][:, :, :], [("kst", gb, p) for p in range(4)], [])
                dma(q["Kr"][:, g * 512:(g + 1) * 512], krT[gb][0:32, :], [("krT", gb, i) for i in range(4)], [])
                dma(q["Vs"][:, :, 4 * g:4 * g + 4, :].rearrange("h p b c -> p h b c"), vst[gb][:, :, :, :],
                    [("vst", gb, i) for i in range(4)], [])

            swpipe(tiles, [sL, None, s1, s2, s3, s4, s5, s6, s7, s8, s9, s10, s11, s12, s13, s14, s15])
            sch.emit()
            if stop == 1:
                return nc
        stAB.close()

        with ExitStack() as st:
            wB = sb(st, "wB", [128, 8, 768], BF)
            wuqn = sb(st, "wuqn", [128, 3, 512], BF)
            wuqr = sb(st, "wuqr", [128, 3, 256], BF)
            wuqrs = sb(st, "wuqrs", [128, 3, 256], BF)
            dma(wB[:, :, :], w_inB_d.rearrange("(k p) n -> p k n", p=128), [], ["wB"], q="pool")
            dma(wuqn[:, :, :], w_uqn_d.rearrange("(k p) n -> p k n", p=128), [], ["wuqn"], q="pool")
            dma(wuqr[:, :, :], w_uqr_d.rearrange("(k p) n -> p k n", p=128), [], ["wuqr"], q="pool")
            dma(wuqrs[:, :, :], w_uqrs_d.rearrange("(k p) n -> p k n", p=128), [], ["wuqrs"], q="pool")
            xt = [sb(st, "xt%d" % i, [128, 1024], F32) for i in range(3)]
            junk = sb(st, "junk", [128, 1024], BF)
            junkb = sb(st, "junkb", [128, 384], BF)
            stat = sb(st, "stat", [128, 16, 4], F32)
            stat2 = sb(st, "stat2", [128, 16, 4], F32)
            xn = [sb(st, "xn%d" % i, [128, 1024], BF) for i in range(2)]
            hTg = [sb(st, "hTg%d" % i, [128, 8, 512], BF) for i in range(2)]
            hTh = [sb(st, "hTh%d" % i, [128, 8, 128], BF) for i in range(2)]
            cqn = [sb(st, "cqn%d" % i, [128, 384], BF) for i in range(2)]
            cqT = [sb(st, "cqT%d" % i, [128, 3, 512], BF) for i in range(2)]
            kwst = [sb(st, "kwst%d" % i, [128, 2, 128], BF) for i in range(2)]
            vwst = [sb(st, "vwst%d" % i, [128, 2, 128], BF) for i in range(2)]
            qst = [sb(st, "qst%d" % i, [128, 4, 512], BF) for i in range(2)]
            qrst = [sb(st, "qrst%d" % i, [128, 2, 512], BF) for i in range(2)]
            qt1 = [sb(st, "qt1_%d" % i, [128, 512], F32) for i in range(2)]
            qt2 = [sb(st, "qt2_%d" % i, [128, 512], F32) for i in range(2)]
            ropeQ = sb(st, "ropeQ", [128, 2, Nqmax], F32)
            pT = [psb(st, "pT%d" % i) for i in range(2)]
            pB = [psb(st, "pB%d" % i) for i in range(2)]
            pKa = psb(st, "pKa")
            pT2 = psb(st, "pT2")
            pQ = [psb(st, "pQ%d" % i) for i in range(2)]
            for b in range(2):
                mset("pool", vwst[b][:, :, 64:128], 1.0, [("vw1", b)])
            NXB = 6
            cqf = [sb(st, "cqf%d" % i, [128, 384], F32) for i in range(3)]
            xtb = xt + [sb(st, "xtb%d" % i, [128, 1024], F32) for i in range(NXB - len(xt))]
            ropeQl = [ropeQ, sb(st, "ropeQ1", [128, 2, seqs[1]["Nq"]], F32)]
            for q in seqs:
                dma(ropeQl[q["n"]][:, :, 0:q["Nq"]], q["ropeQ"], [], [("ropeQ", q["n"])])
            tiles = []
            for q in seqs:
                for u in range(q["NT"]):
                    own = 1 <= u <= q["NT"] - 2
                    o = u - 1
                    tiles.append(dict(i=len(tiles), q=q, n=q["n"], u=u, own=own, g=(o // 4 if own else 0), tl=(o % 4 if own else 0),
                                      hb=(0 if u == 0 else 1)))
            gcount = 0
            for T in tiles:
                if T["own"] and T["tl"] == 0:
                    gcount += 1
                T["gb"] = (gcount - 1) % 2 if T["own"] else 0
            p2 = pT2[:, :].bitcast(BF)
            qc = [0]

            def dstf(T):
                if T["own"]:
                    gb, tl = T["gb"], T["tl"]
                    return (lambda j: hTg[gb][:, j, tl * 128:(tl + 1) * 128]), (lambda j: ("hTg", gb, tl, j))
                hb = T["hb"]
                return (lambda j: hTh[hb][:, j, :]), (lambda j: ("hTh", hb, j))

            def bL(T):
                x = T["i"] % NXB
                dma(xtb[x][:, :], T["q"]["xo"][T["u"] * 128:(T["u"] + 1) * 128, :], [], [("xt", x)])

            def b1(T):
                x, s_ = T["i"] % NXB, T["i"] % 16
                act(junk[:, :], xtb[x][:, :], AF.Square, [("xt", x)], [("st", s_, 0), "junk"], accum_out=stat[:, s_, 0:1])

            def b2(T):
                rstd_ops(stat, T["i"] % 16, 1.0 / 1024, "st")

            def b3(T):
                i = T["i"]
                x, s_, b = i % NXB, i % 16, i % 2
                ts("dve", xn[b][:, :], xtb[x][:, :], stat[:, s_, 2:3], None, ALU.mult, None, [("xt", x), ("st", s_, 2)], [("xn", b)])

            def b4(T):
                b = T["i"] % 2
                pTv = pT[b][:, :].bitcast(BF)
                for j in range(8):
                    tp(pTv[:, j * 128:(j + 1) * 128], xn[b][:, j * 128:(j + 1) * 128], [("xn", b)], [("pT", b)])

            def b5(T):
                b, n = T["i"] % 2, T["n"]
                pTv = pT[b][:, :].bitcast(BF)
                dst, rdst = dstf(T)
                for j in range(8):
                    act(dst(j), pTv[:, j * 128:(j + 1) * 128], AF.Identity, [("pT", b), "a_all", "b_all"], [rdst(j)],
                        scale=a_all[:, j, n:n + 1], bias=b_all[:, j, n:n + 1])

            def b6(T):
                b = T["i"] % 2
                dst, rdst = dstf(T)
                if T["own"]:
                    for j in range(8):
                        mm(pB[b][:, 0:384], dst(j), wB[:, j, 0:384], j == 0, j == 7, [rdst(j), "wB"], [("pB", b)])
                for j in range(8):
                    mm(pKa[:, 0:128], wB[:, j, 384:512], dst(j), j == 0, j == 7, [rdst(j), "wB"], ["pKa"])
                for j in range(8):
                    mm(pKa[:, 128:256], wB[:, j, 512:640], dst(j), j == 0, j == 7, [rdst(j), "wB"], ["pKa"])
                for j in range(8):
                    mm(pKa[:, 256:384], dst(j), wB[:, j, 640:768], j == 0, j == 7, [rdst(j), "wB"], ["pKa"])

            def b7(T):
                i = T["i"]
                b, s_ = i % 2, i % 16
                cp("act", kwst[b][:, :, :], pKa[:, 0:256].rearrange("p (s t) -> p s t", s=2), ["pKa"], [("kwst", b)])
                cp("act", vwst[b][:, :, 0:64], pKa[:, 256:384].rearrange("p (k c) -> p k c", k=2), ["pKa", ("vw1", b)], [("vwst", b)])
                if T["own"]:
                    act(junkb[:, 0:384], pB[b][:, 0:384], AF.Square, [("pB", b)], [("st2", s_, 0), "junkb"], accum_out=stat2[:, s_, 0:1])
                    cp("act", cqf[i % 3][:, :], pB[b][:, 0:384], [("pB", b)], [("cqf", i % 3)])
                    if T["tl"] == 3:
                        q, g, gb = T["q"], T["g"], T["gb"]
                        dma(q["Hs"][:, :, g * 512:(g + 1) * 512].rearrange("j p t -> p j t"), hTg[gb][:, :, :],
                            [("hTg", gb, i_, j) for i_ in range(4) for j in range(8)], [])

            def b8(T):
                b, q, u = T["i"] % 2, T["q"], T["u"]
                if T["own"]:
                    rstd_ops(stat2, T["i"] % 16, 1.0 / 384, "st2")
                dma(q["Kw"][:, :, u * 128:(u + 1) * 128].rearrange("s p t -> p s t"), kwst[b][:, :, :], [("kwst", b)], [])
                dma(q["Vw"][:, u, :], vwst[b][:, :, :].rearrange("p k c -> p (k c)"), [("vwst", b)], [])

            def b9(T):
                if not T["own"]:
                    return
                b, s_ = T["i"] % 2, T["i"] % 16
                stt(cqn[b][:, :], cqf[T["i"] % 3][:, :], stat2[:, s_, 2:3], gq[:, :], ALU.mult, ALU.mult,
                    [("cqf", T["i"] % 3), ("st2", s_, 2), "gq"], [("cqn", b)])

            def b10(T):
                if not T["own"]:
                    return
                b = T["i"] % 2
                for c in range(3):
                    tp(p2[:, c * 128:(c + 1) * 128], cqn[b][:, c * 128:(c + 1) * 128], [("cqn", b)], ["pT2"])

            def b11(T):
                if not T["own"]:
                    return
                gb, tl = T["gb"], T["tl"]
                cp("act", cqT[gb][:, :, tl * 128:(tl + 1) * 128], p2[:, 0:384].rearrange("p (c t) -> p c t", c=3), ["pT2"], [("cqT", gb, tl)])

            def b12(T):
                if not (T["own"] and T["tl"] == 3):
                    return
                q, g, gb, n = T["q"], T["g"], T["gb"], T["n"]
                rall = [("cqT", gb, i_) for i_ in range(4)]
                for p in range(4):
                    pq, rq = pQ[qc[0] % 2], ("pQ", qc[0] % 2)
                    qc[0] += 1
                    for c in range(3):
                        mm(pq[:, :], wuqn[:, c, p * 128:(p + 1) * 128], cqT[gb][:, c, :], c == 0, c == 2, rall + ["wuqn"], [rq])
                    cp("act" if p % 2 == 0 else "dve", qst[gb][:, p, :], pq[:, :], [rq], [("qst", gb, p)])
                    for hh in range(2):
                        dma(q["Qs"][2 * p + hh, 0:64, g * 512:(g + 1) * 512], qst[gb][hh * 64:(hh + 1) * 64, p, :], [("qst", gb, p)], [])
                for rr in range(2):
                    pa, ra = pQ[qc[0] % 2], ("pQ", qc[0] % 2)
                    qc[0] += 1
                    for c in range(3):
                        mm(pa[:, :], wuqr[:, c, rr * 128:(rr + 1) * 128], cqT[gb][:, c, :], c == 0, c == 2, rall + ["wuqr"], [ra])
                    tt("dve", qt1[rr][:, :], pa[:, :], ropeQl[n][:, 0, g * 512:(g + 1) * 512], ALU.mult, [ra, ("ropeQ", n)], [("qt1", rr)])
                    pb_, rb = pQ[qc[0] % 2], ("pQ", qc[0] % 2)
                    qc[0] += 1
                    for c in range(3):
                        mm(pb_[:, :], wuqrs[:, c, rr * 128:(rr + 1) * 128], cqT[gb][:, c, :], c == 0, c == 2, rall + ["wuqrs"], [rb])
                    tt("dve", qt2[rr][:, :], pb_[:, :], ropeQl[n][:, 1, g * 512:(g + 1) * 512], ALU.mult, [rb, ("ropeQ", n)], [("qt2", rr)])
                    tt("pool", qrst[gb][:, rr, :], qt1[rr][:, :], qt2[rr][:, :], ALU.add, [("qt1", rr), ("qt2", rr)], [("qrst", gb, rr)])
                    for i_ in range(4):
                        dma(q["Qs"][4 * rr + i_, 64:96, g * 512:(g + 1) * 512], qrst[gb][i_ * 32:(i_ + 1) * 32, rr, :], [("qrst", gb, rr)], [])

            swpipe(tiles, [bL, None, b1, b2, b3, b4, b5, b6, b7, b8, b9, b10, b11, b12])
            sch.emit()
            if stop == 2:
                return nc

        with ExitStack() as st:
            KT = [sb(st, "KT%d" % i, [96, Smax], BF) for i in range(2)]
            VT = [sb(st, "VT%d" % i, [128, Smax // 128, 128], BF) for i in range(2)]
            QT = [sb(st, "QT%d" % i, [96, Nqmax], BF) for i in range(2)]
            PT = [sb(st, "PT%d" % i, [128, 1024], BF) for i in range(3)]
            rec = [sb(st, "rec%d" % i, [128, 512], F32) for i in range(2)]
            ybst = [sb(st, "ybst%d" % i, [64, 512], BF) for i in range(2)]
            ps_s = [st.enter_context(nc.psum_tensor("ps2_%d" % i, [128, 1024], F32)) for i in range(2)]
            pacc = [psb(st, "pacc%d" % i) for i in range(2)]
            NP = 4
            work = [(q, h) for q in seqs for h in range(8)]

            def loadhead(i):
                q, h = work[i]
                b = i % 2
                S, Nq = q["S"], q["Nq"]
                pc = S // NP
                dma(QT[b][0:96, 0:Nq], q["Qs"][h, :, :], [], [("QT", b)])
                for pi in range(NP):
                    dma(KT[b][0:64, pi * pc:(pi + 1) * pc], q["Ks"][h, :, pi * pc:(pi + 1) * pc], [], [("KTn", b, pi)])
                    dma(KT[b][64:96, pi * pc:(pi + 1) * pc], q["Kr"][:, pi * pc:(pi + 1) * pc], [], [("KTr", b, pi)])
                    nb = pc // 128
                    dma(VT[b][:, pi * nb:(pi + 1) * nb, :], q["Vs"][h, :, pi * nb:(pi + 1) * nb, :], [], [("VT", b, pi)])
            loadhead(0)
            scount = 0
            pcount = 0
            acount = 0
            SC = 96.0 ** -0.5
            for i, (q, h) in enumerate(work):
                if i + 1 < len(work):
                    loadhead(i + 1)
                b = i % 2
                S, Nq = q["S"], q["Nq"]
                nkb = S // 128
                kpp = nkb // NP
                for qg in range(Nq // 512):
                    a = acount % 2
                    acount += 1
                    sidx = {}
                    pidx = {}

                    npair = nkb // 2

                    def smm(p):
                        nonlocal scount
                        si = scount % 2
                        scount += 1
                        sidx[p] = si
                        for h2 in range(2):
                            kb = 2 * p + h2
                            pi = kb // kpp
                            mm(ps_s[si][:, h2 * 512:(h2 + 1) * 512], KT[b][0:96, kb * 128:(kb + 1) * 128],
                               QT[b][0:96, qg * 512:(qg + 1) * 512], True, True,
                               [("KTn", b, pi), ("KTr", b, pi), ("QT", b)], [("ps_s", si)])
                    smm(0)
                    if npair > 1:
                        smm(1)
                    for p in range(npair):
                        si = sidx[p]
                        pj = pcount % 3
                        pcount += 1
                        act(PT[pj][:, :], ps_s[si][:, :], AF.Exp, [("ps_s", si)], [("PT", pj)], scale=SC)
                        if p + 2 < npair:
                            smm(p + 2)
                        for h2 in range(2):
                            kb = 2 * p + h2
                            mm(pacc[a][:, :], VT[b][:, kb, :], PT[pj][:, h2 * 512:(h2 + 1) * 512], kb == 0, kb == nkb - 1,
                               [("VT", b, kb // kpp), ("PT", pj)], [("pacc", a)])
                    recip(rec[a][64:128, :], pacc[a][64:128, :], [("pacc", a)], [("rec", a)])
                    tt("dve", ybst[a][0:64, :], pacc[a][0:64, :], rec[a][64:128, :], ALU.mult, [("pacc", a), ("rec", a)], [("ybst", a)])
                    dma(q["Ybs"][h // 2, (h % 2) * 64:(h % 2) * 64 + 64, qg * 512:(qg + 1) * 512], ybst[a][0:64, :], [("ybst", a)], [])
            sch.emit()
            if stop == 3:
                return nc

        with ExitStack() as st:
            wD = sb(st, "wD", [128, 8, 3584], BF)
            woa = sb(st, "woa", [128, 4, 1024], BF)
            wob = sb(st, "wob", [128, 4, 1024], BF)
            wout = sb(st, "wout", [128, 8, 1024], BF)
            for pc in range(2):
                dma(wD[:, :, pc * 1792:(pc + 1) * 1792],
                    w_inD_d[:, pc * 1792:(pc + 1) * 1792].rearrange("(k p) n -> p k n", p=128), [], [("wD", pc)], q="pool")
            dma(woa[:, :, :], w_oa_d.rearrange("(k p) n -> p k n", p=128), [], ["woa"], q="pool")
            dma(wob[:, :, :], w_ob_d.rearrange("(k p) n -> p k n", p=128), [], ["wob"], q="pool")
            dma(wout[:, :, :], w_out_d.rearrange("(k p) n -> p k n", p=128), [], ["wout"], q="pool")
            rwD = [("wD", 0), ("wD", 1)]
            xr = [sb(st, "xr%d" % i, [128, 1024], F32) for i in range(2)]
            hTg = [sb(st, "hTg%d" % i, [128, 8, 512], BF) for i in range(2)]
            qaT = sb(st, "qaT", [128, 4, 512], BF)
            sz = sb(st, "sz", [128, 8, 512], BF)
            tmpF = [sb(st, "tmpF%d" % i, [128, 512], F32) for i in range(4)]
            tab = [sb(st, "tab%d" % i, [128, 512], BF) for i in range(4)]
            kw = [sb(st, "kw%d" % i, [128, 4, 768], BF) for i in range(2)]
            vw = [sb(st, "vw%d" % i, [128, 6, 256], BF) for i in range(2)]
            ybT = [sb(st, "ybT%d" % i, [128, 4, 512], BF) for i in range(2)]
            PW = [sb(st, "PW%d" % i, [128, 512], BF) for i in range(3)]
            mergedT = sb(st, "mergedT", [128, 8, 512], BF)
            res = [sb(st, "res%d" % i, [128, 1024], F32) for i in range(2)]
            yst = [sb(st, "yst0", [128, 1024], F32)] * 2
            stat = sb(st, "stat", [128, 16, 4], F32)
            wmask = sb(st, "wmask", [128, 2, 2], F32)
            PB = [psb(st, "PB%d" % i) for i in range(8)]
            for q in seqs:
                dma(wmask[:, q["n"], :], q["wmask"], [], ["wmask"])
            glist = [(q, g) for q in seqs for g in range(q["Nq"] // 512)]

            def loadgroup(i):
                q, g = glist[i]
                gb = i % 2
                dma(hTg[gb][:, :, :], q["Hs"][:, :, g * 512:(g + 1) * 512].rearrange("j p t -> p j t"), [], [("hTg", gb)])
                for v in range(4):
                    half, kvh = v % 2, v // 2
                    sel = 0 if half == kvh else 1
                    dma(kw[gb][half * 64:(half + 1) * 64, v, :], q["Kw"][sel, half * 64:(half + 1) * 64, 4 * g * 128:(4 * g + 6) * 128],
                        ["kwz"], [("kw", gb, v)])
                dma(vw[gb][:, :, :], q["Vw"][:, 4 * g:4 * g + 6, :], [], [("vw", gb)])
                dma(ybT[gb][:, :, :], q["Ybs"][:, :, g * 512:(g + 1) * 512].rearrange("c p t -> p c t"), [], [("ybT", gb)])
            for b_ in range(2):
                mset("pool", kw[b_][:, :, :], 0.0, ["kwz"])
            loadgroup(0)
            tf = 0
            wcount = 0
            acount = 0
            ocount = 0
            SCW = 64.0 ** -0.5
            for i, (q, g) in enumerate(glist):
                n, Nq, NT = q["n"], q["Nq"], q["NT"]
                gb = i % 2
                if i + 1 < len(glist):
                    loadgroup(i + 1)
                H = hTg[gb]
                rH = [("hTg", gb)]

                def inproj(col0, bank, rbank):
                    for j in range(8):
                        mm(PB[bank][:, :], wD[:, j, col0:col0 + 128], H[:, j, :], j == 0, j == 7, rH + rwD, [rbank])
                for c in range(4):
                    bk = c % 2
                    inproj(c * 128, bk, ("PB", bk))
                    cp("dve", qaT[:, c, :], PB[bk][:, :], [("PB", bk)], [("qaT", c)])
                for c in range(8):
                    bk = c % 2
                    inproj(512 + c * 128, bk, ("PB", bk))
                    t_ = tf % 4
                    tf += 1
                    act(tmpF[t_][:, :], PB[bk][:, :], AF.Tanh, [("PB", bk)], [("tmpF", t_)], scale=0.5)
                    stt(sz[:, c, :], tmpF[t_][:, :], 1.0, PB[bk][:, :], ALU.add, ALU.mult, [("tmpF", t_), ("PB", bk)], [("sz", c)])
                for c in range(4):
                    tt("pool", ybT[gb][:, c, :], ybT[gb][:, c, :], sz[:, 4 + c, :], ALU.mult, [("ybT", gb), ("sz", 4 + c)], [("ybT", gb)])
                if i == 0:
                    print("D window starts at op", Sched.nops, flush=True)
                for tl in range(4):
                    o = 4 * g + tl
                    for kvh in range(2):
                        a = 4 + (acount % 2)
                        acount += 1
                        for r in range(3):
                            wt = tl + r
                            bk = 2 + (wcount % 2)
                            pw = wcount % 3
                            wcount += 1
                            mm(PB[bk][:, :], ident_bf[:, :], wb[:, (r * 8 + kvh * 4) * 128:(r * 8 + kvh * 4 + 4) * 128],
                               True, False, ["ident", "wb"], [("PB", bk)])
                            for g4 in range(4):
                                hh = kvh * 4 + g4
                                half, chunk = hh % 2, hh // 2
                                v = kvh * 2 + half
                                mm(PB[bk][:, g4 * 128:(g4 + 1) * 128],
                                   kw[gb][:, v, wt * 128:(wt + 1) * 128],
                                   qaT[:, chunk, tl * 128:(tl + 1) * 128],
                                   False, g4 == 3, [("kw", gb, v), ("qaT", chunk)], [("PB", bk)])
                            kwargs = dict(scale=SCW)
                            if o == 0 and r == 0:
                                kwargs["bias"] = wmask[:, n, 0:1]
                            if o == Nq // 128 - 1 and r == 2:
                                kwargs["bias"] = wmask[:, n, 1:2]
                            act(PW[pw][:, :], PB[bk][:, :], AF.Exp, [("PB", bk), "wmask"], [("PW", pw)], **kwargs)
                            mm(PB[a][:, :], vw[gb][:, wt, kvh * 128:(kvh + 1) * 128], PW[pw][:, :], r == 0, r == 2,
                               [("vw", gb), ("PW", pw)], [("PB", a)])
                        t0 = tf % 4
                        t1 = (tf + 1) % 4
                        tf += 2
                        for g4 in range(4):
                            act(tmpF[t0][64:128, g4 * 128:(g4 + 1) * 128], PB[a][64:128, g4 * 128:(g4 + 1) * 128], AF.Identity,
                                [("PB", a), "es"], [("tmpF", t0)], bias=es[64:128, kvh * 4 + g4:kvh * 4 + g4 + 1])
                        recip(tmpF[t0][64:128, :], tmpF[t0][64:128, :], [("tmpF", t0)], [("tmpF", t0)])
                        tt("dve", tmpF[t1][0:64, :], PB[a][0:64, :], tmpF[t0][64:128, :], ALU.mult,
                           [("PB", a), ("tmpF", t0)], [("tmpF", t1)])
                        tt("dve", tmpF[t1][64:128, :], PB[a][0:64, :], tmpF[t0][64:128, :], ALU.mult,
                           [("PB", a), ("tmpF", t0)], [("tmpF", t1)])
                        for g4 in range(4):
                            hh = kvh * 4 + g4
                            half, chunk = hh % 2, hh // 2
                            dsl = sz[half * 64:(half + 1) * 64, chunk, tl * 128:(tl + 1) * 128]
                            tt("pool", dsl, tmpF[t1][half * 64:(half + 1) * 64, g4 * 128:(g4 + 1) * 128], dsl, ALU.mult,
                               [("tmpF", t1), ("sz", chunk)], [("sz", chunk)])
                if i == 0:
                    print("D merge starts at op", Sched.nops, flush=True)
                for oc in range(8):
                    ta = tf % 4
                    tb_ = (tf + 1) % 4
                    tf += 2
                    inproj(1536 + oc * 128, 0, ("PB", 0))
                    act(tab[ta][:, :], PB[0][:, :], AF.Tanh, [("PB", 0)], [("tab", ta)], scale=0.5)
                    inproj(1536 + 1024 + oc * 128, 1, ("PB", 1))
                    act(tab[tb_][:, :], PB[1][:, :], AF.Tanh, [("PB", 1)], [("tab", tb_)], scale=0.5)
                    for c in range(4):
                        mm(PB[2][:, :], woa[:, c, oc * 128:(oc + 1) * 128], sz[:, c, :], c == 0, c == 3, [("sz", c), "woa"], [("PB", 2)])
                    for c in range(4):
                        mm(PB[3][:, :], wob[:, c, oc * 128:(oc + 1) * 128], ybT[gb][:, c, :], c == 0, c == 3, [("ybT", gb), "wob"], [("PB", 3)])
                    stt(tmpF[ta][:, :], tab[ta][:, :], 1.0, PB[2][:, :], ALU.add, ALU.mult, [("tab", ta), ("PB", 2)], [("tmpF", ta)])
                    stt(tmpF[tb_][:, :], tab[tb_][:, :], 1.0, PB[3][:, :], ALU.add, ALU.mult, [("tab", tb_), ("PB", 3)], [("tmpF", tb_)])
                    tt("pool", mergedT[:, oc, :], tmpF[ta][:, :], tmpF[tb_][:, :], ALU.add, [("tmpF", ta), ("tmpF", tb_)], [("mT", oc)])
                def out_tail(o_, ob_, s_, q=q):
                    stt(yst[ob_][:, :], res[ob_][:, :], stat[:, s_, 2:3], gf[:, :], ALU.mult, ALU.mult,
                        [("res", ob_), ("st", s_, 2), "gf"], [("yst", 0)])
                    dma(q["y"][o_ * 128:(o_ + 1) * 128, :], yst[ob_][:, :], [("yst", 0)], [])
                pend = None
                for tl in range(4):
                    o = 4 * g + tl
                    ob = ocount % 2
                    s = ocount % 16
                    ocount += 1
                    dma(xr[ob][:, :], q["xo"][(o + 1) * 128:(o + 2) * 128, :], [], [("xr", ob)])
                    for half in range(2):
                        bk = 6 + half
                        for c in range(8):
                            mm(PB[bk][:, :], mergedT[:, c, tl * 128:(tl + 1) * 128], wout[:, c, half * 512:(half + 1) * 512],
                               c == 0, c == 7, [("mT", c), "wout"], [("PB", bk)])
                        tt("dve", res[ob][:, half * 512:(half + 1) * 512], PB[bk][:, :], gate_bc[:, n, half * 512:(half + 1) * 512],
                           ALU.mult, [("PB", bk), "gate_bc"], [("res", ob)])
                    if pend is not None:
                        out_tail(*pend)
                    tt("pool", res[ob][:, :], res[ob][:, :], xr[ob][:, :], ALU.add, [("res", ob), ("xr", ob)], [("res", ob)])
                    act(qaT[:, 0:2, :].rearrange("p c t -> p (c t)"), res[ob][:, :], AF.Square, [("res", ob)],
                        [("st", s, 0), ("qaT", 0), ("qaT", 1)], accum_out=stat[:, s, 0:1])
                    rstd_ops(stat, s, 1.0 / 1024, "st")
                    pend = (o, ob, s)
                out_tail(*pend)
            sch.emit()
            if stop == 4:
                return nc
    return nc


def _rope_tables(S):
    inv = (1.0 / (np.float32(10000.0) ** (np.arange(0, 32, 2, dtype=np.float32) / np.float32(32)))).astype(np.float32)
    ang = (np.arange(S, dtype=np.float32)[:, None] * inv[None, :]).astype(np.float32)
    cos = np.cos(ang).astype(np.float32)
    sin = np.sin(ang).astype(np.float32)
    C = np.concatenate([cos, cos], 1)
    Sn = np.concatenate([-sin, sin], 1)
    return C, Sn


def _wbias():
    slopes = (2.0 ** (-8.0 / 8)) ** np.arange(1, 9, dtype=np.float64)
    out = np.zeros((128, 3, 8, 128), np.float32)
    j = np.arange(128)[:, None]
    i = np.arange(128)[None, :]
    for r in range(3):
        rel = (r * 128 + j) - 128 - i
        valid = np.abs(rel) <= 128
        for h in range(8):
            b = -slopes[h] * np.abs(rel) * 8.0
            out[:, r, h, :] = np.where(valid, b, -240000.0)
    return out.reshape(128, 3072)


def _prep(inputs, S_s, S_p):
    f = lambda a: np.ascontiguousarray(np.asarray(a, dtype=np.float32))
    x_s, x_p = f(inputs["x_sample"]), f(inputs["x_prompt"])
    c_s, c_p = f(inputs["c_sample"]), f(inputs["c_prompt"])
    w_in = f(inputs["w_in"])[0]
    w_ukv = f(inputs["w_ukv"])[0].reshape(256, 8, 128)
    w_uq = f(inputs["w_uq"])[0].reshape(384, 8, 96)
    qa, ka, va, za = w_in[:, 0:512], w_in[:, 512:640], w_in[:, 640:768], w_in[:, 768:1280]
    cq, ckv, kr = w_in[:, 1280:1664], w_in[:, 1664:1920], w_in[:, 1920:1952]
    zb, gm = w_in[:, 1952:2464], w_in[:, 2464:4512]
    krsw = np.concatenate([kr[:, 16:32], kr[:, 0:16]], 1)
    kasw = np.concatenate([ka[:, 64:128], ka[:, 0:64]], 1)
    uqr = w_uq[:, :, 64:96]
    uqrs = np.concatenate([uqr[:, :, 16:32], uqr[:, :, 0:16]], 2)
    shared = {
        "w_ada": f(inputs["w_ada"])[0],
        "b_adaT": f(f(inputs["b_ada"])[0].reshape(24, 128).T),
        "b_gate": f(f(inputs["b_ada"])[0][2048:3072].reshape(1, 1024)),
        "b_scale": f(f(inputs["b_ada"])[0][1024:2048].reshape(1, 1024)),
        "gn_bc": f(np.broadcast_to(f(inputs["g_norm"])[0][None, :], (128, 1024))),
        "g_normT": f(f(inputs["g_norm"])[0].reshape(8, 128).T),
        "w_inA": f(np.concatenate([ckv, kr, krsw], 1)),
        "w_inB": f(np.concatenate([cq, ka, kasw, va], 1)),
        "w_inD": f(np.concatenate([qa, za, zb, gm], 1)),
        "gkv_bc": f(np.broadcast_to(f(inputs["g_kv"])[0][None, :], (128, 256))),
        "gq_bc": f(np.broadcast_to(f(inputs["g_q"])[0][None, :], (128, 384))),
        "gf_bc": f(np.broadcast_to(f(inputs["g_final"])[None, :], (128, 1024))),
        "w_uk": f(w_ukv[:, :, 0:64].reshape(256, 512)),
        "w_uv": f(w_ukv[:, :, 64:128].reshape(256, 512)),
        "w_uqn": f(w_uq[:, :, 0:64].reshape(384, 512)),
        "w_uqr": f(uqr.reshape(384, 256)),
        "w_uqrs": f(uqrs.reshape(384, 256)),
        "sink_bc": f(np.broadcast_to(f(inputs["sink"])[0][None, :], (128, 8))),
        "w_oa": f(inputs["w_oa"])[0],
        "w_ob": f(inputs["w_ob"])[0],
        "w_out": f(inputs["w_out"])[0],
        "wbias": _wbias(),
        "ident": np.eye(128, dtype=np.float32),
    }
    tabs = {}
    for nm, S in (("s", S_s), ("p", S_p)):
        C, Sn = _rope_tables(S)
        tabs[nm] = (C, Sn, f(np.concatenate([C, Sn], 1)))
    maps = []
    for c in range(NCORES):
        m = dict(shared)
        cs = [c_s[c // 4], c_p[c // 2]]
        m["cT"] = f(np.stack([v.reshape(8, 128).T for v in cs], -1))
        for nm, S, x, bidx, part, nparts in (("s", S_s, x_s, c // 4, c % 4, 4), ("p", S_p, x_p, c // 2, c % 2, 2)):
            Nq = S // nparts
            q0 = part * Nq
            C, Sn, CS = tabs[nm]
            xs = x[bidx]
            m["xf_" + nm] = xs
            xo = np.zeros((Nq + 256, 1024), np.float32)
            lo, hi = max(q0 - 128, 0), min(q0 + Nq + 128, S)
            xo[lo - (q0 - 128):hi - (q0 - 128)] = xs[lo:hi]
            m["xo_" + nm] = xo
            m["ropeK_" + nm] = CS
            rq = np.stack([np.tile(C[q0:q0 + Nq].T, (4, 1)), np.tile(Sn[q0:q0 + Nq].T, (4, 1))], 1)
            m["ropeQ_" + nm] = f(rq)
            wm = np.zeros((128, 2), np.float32)
            if q0 == 0:
                wm[:, 0] = -30000.0
            if q0 + Nq == S:
                wm[:, 1] = -30000.0
            m["wmask_" + nm] = wm
        maps.append(m)
    return maps


_NC_CACHE = {}


def kernel(**inputs):
    S_s = int(np.shape(inputs["x_sample"])[1])
    S_p = int(np.shape(inputs["x_prompt"])[1])
    maps = _prep(inputs, S_s, S_p)
    import os
    if os.environ.get("K_MAXOPS"):
        Sched.maxops = int(os.environ["K_MAXOPS"])
    stop = os.environ.get("K_STOP")
    stop = int(stop) if stop else None
    key = (S_s, S_p, stop)
    if key not in _NC_CACHE:
        _NC_CACHE[key] = build(S_s, S_p, stop)
    nc = _NC_CACHE[key]
    ncr = int(os.environ.get("K_NCORES", NCORES))
    print("total ops recorded", Sched.nops, flush=True)
    c0 = int(os.environ.get('K_C0', 0))
    res = run_bass_kernel_spmd(nc, maps[c0:c0 + ncr], core_ids=list(range(ncr)))
    if ncr < NCORES:
        return res
    B_p = np.shape(inputs["x_prompt"])[0]
    B_s = np.shape(inputs["x_sample"])[0]
    y_p = np.zeros((B_p, S_p, 1024), np.float32)
    y_s = np.zeros((B_s, S_s, 1024), np.float32)
    for c in range(NCORES):
        r = res.results[c]
        Nq = S_s // 4
        y_s[c // 4, (c % 4) * Nq:(c % 4 + 1) * Nq] = np.asarray(r["y_s"], dtype=np.float32)
        Nq = S_p // 2
        y_p[c // 2, (c % 2) * Nq:(c % 2 + 1) * Nq] = np.asarray(r["y_p"], dtype=np.float32)
    return (y_p, y_s)
```
